# Optimizing a Trainium2 kernel written in Bass

```python
import math
import jax, jax.numpy as jnp
from jax import lax
import numpy as np

D_MODEL = 1024
BATCH = 8
SEQ = 4096
DEPTH = 1

MIX_WIDTH = D_MODEL
POOL_WIDTH = MIX_WIDTH // 2
SGU_WIDTH = MIX_WIDTH - POOL_WIDTH
POOL_WINDOWS = (2, 4, 8, 16)
N_POOL_GROUPS = len(POOL_WINDOWS)
POOL_GROUP_DIM = POOL_WIDTH // N_POOL_GROUPS
SGU_HEADS = 4
SGU_HEAD_DIM = SGU_WIDTH // SGU_HEADS
SGU_CHUNK = 128
IN_WIDTH = POOL_WIDTH + 2 * SGU_WIDTH
PEER_HEADS = 8
PEER_N_KEYS = 128
PEER_N_EXPERTS = PEER_N_KEYS * PEER_N_KEYS
PEER_D_QUERY = 256
PEER_D_HALF = PEER_D_QUERY // 2
PEER_TOPK = 16
PEER_TOKEN_BLOCK = 128
NORM_EPS = 1e-6

kernel_name = "hybrid_pool_sgu_peer_block"


def _rmsnorm(x, g):
    xf = x.astype(jnp.float32)
    inv = lax.rsqrt(jnp.mean(xf * xf, axis=-1, keepdims=True) + NORM_EPS)
    return (xf * inv).astype(x.dtype) * g


def _layernorm(x, g, b):
    xf = x.astype(jnp.float32)
    mu = jnp.mean(xf, axis=-1, keepdims=True)
    var = jnp.mean(jnp.square(xf - mu), axis=-1, keepdims=True)
    return ((xf - mu) * lax.rsqrt(var + NORM_EPS)).astype(x.dtype) * g + b


def _pool_mixer(p, w_pool, pool_scale):
    B, S, _ = p.shape
    pg = p.reshape(B, S, N_POOL_GROUPS, POOL_GROUP_DIM)
    cs = jnp.cumsum(pg.astype(jnp.float32), axis=1)
    t = jnp.arange(S)
    means = []
    for g, win in enumerate(POOL_WINDOWS):
        c = cs[:, :, g]
        lag = jnp.pad(c, ((0, 0), (win, 0), (0, 0)))[:, :S]
        cnt = jnp.minimum(t + 1, win).astype(jnp.float32)[None, :, None]
        means.append((c - lag) / cnt)
    mean = jnp.stack(means, axis=2)
    d = (mean - pg.astype(jnp.float32)).astype(p.dtype)
    out = jnp.einsum('bsgc,gcd->bsgd', d, w_pool) * pool_scale
    return out.reshape(B, S, POOL_WIDTH)


def _spatial_gating(u, v, ln_g, ln_b, w_s, b_s):
    B, S, _ = u.shape
    nc = S // SGU_CHUNK
    u = u.reshape(B, nc, SGU_CHUNK, SGU_HEADS, SGU_HEAD_DIM)
    v = v.reshape(B, nc, SGU_CHUNK, SGU_HEADS, SGU_HEAD_DIM)
    v = _layernorm(v, ln_g, ln_b)
    mask = jnp.tril(jnp.ones((SGU_CHUNK, SGU_CHUNK), dtype=bool))
    w = jnp.where(mask[None], w_s, jnp.zeros_like(w_s))
    mixed = jnp.einsum('hts,bnshc->bnthc', w, v) + b_s.T[None, None, :, :, None]
    return (u * mixed).reshape(B, S, SGU_WIDTH)


def _peer(h, w_q, keys, u_tab, v_tab):
    B, S, D = h.shape
    T = B * S
    xs = h.reshape(T // PEER_TOKEN_BLOCK, PEER_TOKEN_BLOCK, D)

    def block(xc):
        q = (xc @ w_q).reshape(PEER_TOKEN_BLOCK, PEER_HEADS, 2, PEER_D_HALF)
        s = jnp.einsum('chpd,pkd->chpk', q, keys)
        s1, i1 = lax.top_k(s[:, :, 0], PEER_TOPK)
        s2, i2 = lax.top_k(s[:, :, 1], PEER_TOPK)
        cand = (s1[..., :, None] + s2[..., None, :]).reshape(PEER_TOKEN_BLOCK, PEER_HEADS, PEER_TOPK * PEER_TOPK)
        cv, ci = lax.top_k(cand, PEER_TOPK)
        a = jnp.take_along_axis(i1, ci // PEER_TOPK, axis=-1)
        b = jnp.take_along_axis(i2, ci % PEER_TOPK, axis=-1)
        expert = (a * PEER_N_KEYS + b).reshape(PEER_TOKEN_BLOCK, PEER_HEADS * PEER_TOPK)
        gate = jax.nn.softmax(cv.astype(jnp.float32), axis=-1).astype(xc.dtype)
        gate = gate.reshape(PEER_TOKEN_BLOCK, PEER_HEADS * PEER_TOPK)
        u_sel = jnp.take(u_tab, expert, axis=0)
        v_sel = jnp.take(v_tab, expert, axis=0)
        act = jax.nn.gelu(jnp.einsum('cd,ckd->ck', xc, u_sel), approximate=False)
        return jnp.einsum('ck,ckd->cd', gate * act, v_sel)

    out = lax.map(block, xs)
    return out.reshape(B, S, D)


def setup_inputs(seed: int = 0) -> dict:
    key = jax.random.key(seed)
    ks = jax.random.split(key, 20)
    f32 = jnp.float32
    L = DEPTH

    def nrm(k, shape, scale):
        return jax.random.normal(k, shape, f32) * scale

    def gain(k, shape):
        return 1.0 + 0.02 * jax.random.normal(k, shape, f32)

    return {
        "x": jax.random.normal(ks[0], (BATCH, SEQ, D_MODEL), f32),
        "norm_mix": gain(ks[1], (L, D_MODEL)),
        "w_in": nrm(ks[2], (L, D_MODEL, IN_WIDTH), D_MODEL ** -0.5),
        "pool_w": nrm(ks[3], (L, N_POOL_GROUPS, POOL_GROUP_DIM, POOL_GROUP_DIM), POOL_GROUP_DIM ** -0.5),
        "pool_scale": gain(ks[4], (L, N_POOL_GROUPS, POOL_GROUP_DIM)),
        "sgu_ln_g": gain(ks[5], (L, SGU_HEADS, SGU_HEAD_DIM)),
        "sgu_ln_b": nrm(ks[6], (L, SGU_HEADS, SGU_HEAD_DIM), 0.02),
        "sgu_w": nrm(ks[7], (L, SGU_HEADS, SGU_CHUNK, SGU_CHUNK), SGU_CHUNK ** -0.5),
        "sgu_b": gain(ks[8], (L, SGU_HEADS, SGU_CHUNK)),
        "out_norm_pool": gain(ks[9], (L, POOL_WIDTH)),
        "out_norm_sgu": gain(ks[10], (L, SGU_WIDTH)),
        "w_out": nrm(ks[11], (L, MIX_WIDTH, D_MODEL), MIX_WIDTH ** -0.5),
        "norm_ffn": gain(ks[12], (L, D_MODEL)),
        "peer_wq": nrm(ks[13], (L, D_MODEL, PEER_HEADS * PEER_D_QUERY), D_MODEL ** -0.5),
        "peer_keys": nrm(ks[14], (L, 2, PEER_N_KEYS, PEER_D_HALF), PEER_D_HALF ** -0.5),
        "peer_u": nrm(ks[15], (L, PEER_N_EXPERTS, D_MODEL), D_MODEL ** -0.5),
        "peer_v": nrm(ks[16], (L, PEER_N_EXPERTS, D_MODEL), PEER_TOPK ** -0.5),
        "norm_final": gain(ks[17], (D_MODEL,)),
    }


def reference(x, norm_mix, w_in, pool_w, pool_scale, sgu_ln_g, sgu_ln_b, sgu_w, sgu_b,
              out_norm_pool, out_norm_sgu, w_out, norm_ffn, peer_wq, peer_keys, peer_u, peer_v,
              norm_final):
    for l in range(DEPTH):
        h = _rmsnorm(x, norm_mix[l])
        z = h @ w_in[l]
        p = z[..., :POOL_WIDTH]
        gz = jax.nn.gelu(z[..., POOL_WIDTH:], approximate=False)
        gu = gz[..., :SGU_WIDTH]
        gv = gz[..., SGU_WIDTH:]
        a_out = _pool_mixer(p, pool_w[l], pool_scale[l])
        b_out = _spatial_gating(gu, gv, sgu_ln_g[l], sgu_ln_b[l], sgu_w[l], sgu_b[l])
        mixed = jnp.concatenate([_rmsnorm(a_out, out_norm_pool[l]),
                                 _rmsnorm(b_out, out_norm_sgu[l])], axis=-1)
        x = x + mixed @ w_out[l]
        h2 = _rmsnorm(x, norm_ffn[l])
        x = x + _peer(h2, peer_wq[l], peer_keys[l], peer_u[l], peer_v[l])
    return _rmsnorm(x, norm_final)
```

```python
import numpy as np
import concourse.bass as bass
import concourse.mybir as mybir
from concourse.bass_utils import run_bass_kernel_spmd

F32 = mybir.dt.float32
BF16 = mybir.dt.bfloat16
U32 = mybir.dt.uint32
ALU = mybir.AluOpType
AF = mybir.ActivationFunctionType
AX = mybir.AxisListType

ENGS = ("pe", "act", "dve", "pool", "sp")
P = 128
D = 1024
TG = 256
NA = 128
EPS = 1e-6
NEG = -1.0e30


class _Op:
    __slots__ = ("eng", "fn", "deps", "waits", "seq", "dma_sem", "dma_val", "needed", "nofuse")


class Prog:
    def __init__(self, nc):
        self.nc = nc
        self.glob = []
        self.last_writer = {}
        self.readers = {}
        self.eng_sem = {}
        self.dma_sems = {}
        for e in ("pe", "act", "dve", "pool"):
            self.eng_sem[e] = nc.alloc_semaphore("sem_" + e)

    def _dma_sem(self, key):
        if key not in self.dma_sems:
            self.dma_sems[key] = [self.nc.alloc_semaphore("dsem_%d" % len(self.dma_sems)), 0]
        return self.dma_sems[key]

    def add(self, eng, fn, reads=(), writes=(), dma_key=None, excl=(), nofuse=False):
        op = _Op()
        op.nofuse = nofuse
        op.eng = eng
        op.fn = fn
        op.dma_sem = None
        op.dma_val = 0
        op.seq = 0
        op.needed = False
        op.waits = []
        deps = []
        writes = tuple(writes) + tuple(excl)
        for r in reads:
            w = self.last_writer.get(r)
            if w is not None:
                deps.append(w)
        for r in writes:
            w = self.last_writer.get(r)
            if w is not None:
                deps.append(w)
            deps.extend(self.readers.get(r, ()))
        op.deps = deps
        if dma_key is not None:
            s = self._dma_sem(dma_key)
            s[1] += 16
            op.dma_sem = s[0]
            op.dma_val = s[1]
        for r in reads:
            if r in writes:
                continue
            self.readers.setdefault(r, []).append(op)
        for r in writes:
            self.last_writer[r] = op
            self.readers[r] = []
        self.glob.append(op)
        return op

    def finalize(self):
        last = {}
        for op in self.glob:
            last[op.eng] = op
            for d in op.deps:
                if d.dma_sem is None and d.eng == "pe" and op.eng == "pe":
                    continue
                d.needed = True
        for e, op in last.items():
            op.needed = True
        count = {e: 0 for e in ENGS}
        waited = {e: {} for e in ENGS}
        self.ops = {e: [] for e in ENGS}
        for op in self.glob:
            waits = {}
            for d in op.deps:
                if d.dma_sem is not None:
                    key, sem, val = ("d", id(d.dma_sem)), d.dma_sem, d.dma_val
                else:
                    if d.eng == "pe" and op.eng == "pe":
                        continue
                    key, sem, val = ("e", d.eng), self.eng_sem[d.eng], d.seq
                if waited[op.eng].get(key, 0) >= val:
                    continue
                waited[op.eng][key] = val
                waits[key] = (sem, val)
            op.waits = list(waits.values())
            if op.dma_sem is None and op.needed:
                count[op.eng] += 1
                op.seq = count[op.eng]
            self.ops[op.eng].append(op)
        self.count = count

    def emit(self):
        nc = self.nc
        prog = self
        self.finalize()

        def run(engname, e):
            for op in prog.ops[engname]:
                fuse = bool(op.waits) and not op.nofuse and op.dma_sem is None
                sep = op.waits[:-1] if fuse else op.waits
                for (sem, val) in sep:
                    e.wait_ge(sem, val)
                n0 = nc.n_instructions()
                ins = op.fn(e)
                if fuse:
                    assert nc.n_instructions() - n0 == 1, ("multi-instruction op needs nofuse=True", engname)
                    ins._wait_ge(*op.waits[-1])
                if op.dma_sem is not None:
                    ins.then_inc(op.dma_sem, 16)
                elif op.needed:
                    ins.then_inc(prog.eng_sem[engname], 1)
            if engname == "sp":
                for en in ("pe", "act", "dve", "pool"):
                    if prog.count[en]:
                        e.wait_ge(prog.eng_sem[en], prog.count[en])
                for key, (sem, cnt) in prog.dma_sems.items():
                    if cnt:
                        e.wait_ge(sem, cnt)

        with nc.Block() as block:
            @block.tensor
            def _(e):
                run("pe", e)

            @block.scalar
            def _(e):
                run("act", e)

            @block.vector
            def _(e):
                run("dve", e)

            @block.gpsimd
            def _(e):
                run("pool", e)

            @block.sync
            def _(e):
                run("sp", e)


def gkeys(byte_off, nbytes):
    return [("G", b) for b in range(byte_off // 1024, (byte_off + nbytes - 1) // 1024 + 1)]


def build_nc(NT, dbg=False, n_groups=None, na_main=NA):
    nc = bass.Bass("TRN2", target_bir_lowering=False)
    NGRP = NT // TG if n_groups is None else n_groups

    def din(name, shape, dt=F32):
        return nc.dram_tensor(name, list(shape), dt, kind="ExternalInput").ap()

    x_d = din("x", [NT, D])
    win_d = din("w_in", [D, 1536])
    wout_d = din("w_out", [D, D])
    wq_d = din("wq", [D, 2048])
    pu_d = din("peer_u", [16384, D])
    pv_d = din("peer_v", [16384, D])
    ncols_d = din("ncols", [P, 24])
    rows_d = din("rows512", [P, 3, 512])
    nfin_d = din("nfin", [P, D])
    poolw_d = din("poolw", [P, 4, 128])
    sguwT_d = din("sguwT", [P, 4, 128])
    sgub_d = din("sgub", [P, 4])
    keysT_d = din("keysT", [P, 2, 128])
    cm_d = din("cm", [P, 14, 128])
    out_d = nc.dram_tensor("out", [NT, D], F32, kind="ExternalOutput").ap()

    uts_d = nc.dram_tensor("uts", [NA, P, 8, 128], BF16).ap()
    vs_d = nc.dram_tensor("vs", [NA, P, D], BF16).ap()
    wqs_d = nc.dram_tensor("wqs", [4, P, 8, 512], BF16).ap()

    dbg_d = {}
    if dbg:
        for nm, shp in (("d_x2", [TG, D]), ("d_h2T", [P, 8 * TG]), ("d_abg", [P, 3 * TG]),
                        ("d_top", [P, 2 * 256]), ("d_cv", [P, 128]), ("d_G", [P, 8 * 128]),
                        ("d_aout", [P, 512]), ("d_bout", [P, 512])):
            dbg_d[nm] = nc.dram_tensor(nm, shp, F32, kind="ExternalOutput").ap()

    def sb(name, shape, dt=F32):
        return nc.alloc_sbuf_tensor("s_" + name, list(shape), dt)

    winb = sb("winb", [P, 8, 1536], BF16)
    woutb = sb("woutb", [P, 8, 1024], BF16)
    G = sb("G", [P, TG * 128], BF16)
    cm = sb("cm", [P, 14, 128])
    rows = sb("rows", [P, 3, 512])
    nfin = sb("nfin", [P, D])
    ncols = sb("ncols", [P, 24])
    poolwb = sb("poolwb", [P, 4, 128], BF16)
    sguwb = sb("sguwb", [P, 4, 128], BF16)
    sgub = sb("sgub", [P, 4])
    keysT = sb("keysT", [P, 2, 128])
    identb = sb("identb", [P, 128], BF16)
    iota = sb("iota", [P, 128])
    xg2 = sb("xg", [P, 2, 2, D])
    h2T2 = sb("h2T", [P, 2, 8, TG], BF16)
    tb16 = sb("tb16", [P, D], BF16)
    hT = sb("hT", [P, 8, 128], BF16)
    p_sb = sb("p_sb", [P, 2, 512])
    atmp = sb("atmp", [P, 4, 512])
    u_sb = atmp[:, 0, :]
    v_sb = atmp[:, 1, :]
    tmpA = atmp[:, 2, :]
    tmpB = atmp[:, 3, :]
    dTb = sb("dTb", [P, 512], BF16)
    vnb = sb("vnb", [P, 512], BF16)
    st = sb("st", [P, 64])
    utx = sb("utx", [P, 512])
    vtx = sb("vtx", [P, 512])
    arB1 = sb("arB1", [P, 2048])
    arB2 = sb("arB2", [P, 2048])
    abg = sb("abg", [P, 3, TG])
    CB = 4
    CR = 3
    OH = sb("OH", [P, CR, CB, 2, 128], BF16)
    RR = 2
    rep = sb("rep", [P, RR, CB, 3, 128], BF16)
    iota_rep = sb("iota_rep", [P, CB, 2, 128], BF16)
    gl = sb("gl", [P, 2, TG])
    ga = sb("ga", [P, 2, TG], BF16)

    ps = [nc.alloc_psum_tensor("ps%d" % i, [P, 512], F32) for i in range(8)]

    def PB(i):
        return ("ps", i)

    pg = Prog(nc)
    sink = [None]

    def A(*args, **kw):
        if sink[0] is None:
            pg.add(*args, **kw)
        else:
            sink[0].append((args, kw))

    def merge(l1, l2, lead=0):
        n1, n2 = len(l1), len(l2)
        k2 = 0
        lead = min(lead, max(n1 - 1, 0))
        for k1, (a_, kw_) in enumerate(l1):
            pg.add(*a_, **kw_)
            tgt = ((k1 + 1 - lead) * n2) // (n1 - lead) if k1 >= lead else 0
            while k2 < tgt:
                pg.add(*l2[k2][0], **l2[k2][1])
                k2 += 1
        while k2 < n2:
            pg.add(*l2[k2][0], **l2[k2][1])
            k2 += 1


    ident = cm[:, 13, :]

    def gview(byte_off, nbytes, dt):
        e0, n = byte_off // 2, nbytes // 2
        v = G[:, e0:e0 + n]
        return v if dt == BF16 else v.bitcast(dt)

    NSR = 3
    ustage = [(gview(o, 4096, F32), gkeys(o, 4096)) for o in range(0, 12288, 4096)]
    vstage = [(gview(o, 4096, F32), gkeys(o, 4096)) for o in range(12288, 24576, 4096)]
    utbuf = [(gview(o, 2048, BF16), gkeys(o, 2048)) for o in range(24576, 30720, 2048)]
    vbuf = [(gview(o, 2048, BF16), gkeys(o, 2048)) for o in range(30720, 36864, 2048)]
    wstage = [(gview(o, 8192, F32), gkeys(o, 8192)) for o in (36864, 45056)]
    wtmp = [(gview(o, 4096, BF16), gkeys(o, 4096)) for o in (53248, 57344)]
    poolw32 = gview(61440, 2048, F32).rearrange("p (g d) -> p g d", g=4)
    sguw32 = gview(63488, 2048, F32).rearrange("p (g d) -> p g d", g=4)
    def gtmp(off, shape, dt=F32):
        n = 4 * int(np.prod(shape))
        v = gview(off, n, dt)
        names = "abcde"[:len(shape)]
        pat = "p (" + " ".join(names) + ") -> p " + " ".join(names)
        return v.rearrange(pat, **{k: int(d) for k, d in zip(names, shape)}) if len(shape) > 1 else v
    top = gtmp(32768, [2, 8, 2, 16])
    idx = gtmp(34816, [2, 8, 2, 16], U32)
    idxF = gtmp(36864, [8, 2, 16])
    s2r = gtmp(37888, [2, 128])
    cand2 = gtmp(38912, [2, 256])
    cv = gtmp(40960, [8, 16])
    ci = gtmp(41984, [8, 16], U32)
    ge = gtmp(43008, [8, 16])
    gate = gtmp(44032, [8, 16])
    gs = gtmp(45056, [16])
    selu = gtmp(46080, [2, 128], U32)
    selF = gtmp(47104, [2, 128])
    abF = gtmp(48128, [2, 128])
    BTMPK = gkeys(32768, 16384)
    BKEYS = ([("top", j, c) for j in range(2) for c in range(16)] + [("idx", j, c) for j in range(2) for c in range(16)]
             + [("s2r", r) for r in range(2)] + [("cand2", r) for r in range(2)] + [("cv", h) for h in range(8)]
             + [("ci", h) for h in range(8)] + ["ge", "gs", "gate", "selu", "selF", "idxF", ("abF", 0), ("abF", 1)])
    bar = sb("bar", [P, 8])
    PW32K = gkeys(61440, 2048)
    SW32K = gkeys(63488, 2048)

    def ld(dst_ap, src_ap, key):
        A("sp", lambda e: e.dma_start(out=dst_ap, in_=src_ap), writes=[key], dma_key=key)

    ld(cm[:], cm_d, "cm")
    ld(rows[:], rows_d, "rows")
    ld(nfin[:], nfin_d, "nfin")
    ld(ncols[:], ncols_d, "ncols")
    A("sp", lambda e: e.dma_start(out=poolw32, in_=poolw_d), writes=PW32K, dma_key="poolw32")
    A("sp", lambda e: e.dma_start(out=sguw32, in_=sguwT_d), writes=SW32K, dma_key="sguw32")
    ld(sgub[:], sgub_d, "sgub")
    ld(keysT[:], keysT_d, "keysT")
    A("pool", lambda e: e.iota(iota[:], pattern=[[1, 128]], base=0, channel_multiplier=0,
                               allow_small_or_imprecise_dtypes=True), writes=["iota"])
    mhalf = sb("mhalf", [P, 4])
    A("pool", lambda e: e.memset(mhalf[:], -0.5), writes=["mhalf"])
    A("dve", lambda e: e.tensor_copy(out=identb[:], in_=ident), reads=["cm"], writes=["identb"])
    A("dve", lambda e: e.tensor_copy(out=iota_rep[:], in_=iota[:].unsqueeze(1).unsqueeze(1).to_broadcast(
        [P, CB, 2, 128])), reads=["iota"], writes=["iota_rep"])
    A("dve", lambda e: e.tensor_tensor(out=poolwb[:], in0=poolw32,
                                       in1=rows[:, 0, :].rearrange("p (g d) -> p g d", g=4), op=ALU.mult),
      reads=PW32K + ["rows"], writes=["poolwb"])
    A("dve", lambda e: e.tensor_tensor(out=sguwb[:], in0=sguw32,
                                       in1=cm[:, 12, :].unsqueeze(1).to_broadcast([P, 4, 128]), op=ALU.mult),
      reads=SW32K + ["cm"], writes=["sguwb"])

    def load_scaled(src_d, ncol0, width, dst_fn, i0):
        for kk in range(8):
            stg, skeys = wstage[(i0 + kk) % 2]
            sv = stg[:, 0:width]
            A("sp", lambda e, sv=sv, kk=kk: e.dma_start(out=sv, in_=src_d[kk * 128:(kk + 1) * 128, :]),
              writes=skeys, dma_key=("wstage", (i0 + kk) % 2))
            dst_fn(kk, sv, skeys)

    def win_dst(kk, sv, skeys):
        A("dve", lambda e: e.tensor_scalar(out=winb[:, kk, :], in0=sv, scalar1=ncols[:, kk:kk + 1], scalar2=None,
                                           op0=ALU.mult), reads=skeys + ["ncols"], writes=[("winb", kk)])

    def wout_dst(kk, sv, skeys):
        A("dve", lambda e: e.tensor_scalar(out=woutb[:, kk, :], in0=sv, scalar1=ncols[:, 8 + kk:9 + kk],
                                           scalar2=None, op0=ALU.mult),
          reads=skeys + ["ncols"], writes=[("woutb", kk)])

    def wq_dst(kk, sv, skeys):
        tv, tkeys = wtmp[kk % 2]
        A("dve", lambda e: e.tensor_scalar(out=tv, in0=sv, scalar1=ncols[:, 16 + kk:17 + kk], scalar2=None,
                                           op0=ALU.mult), reads=skeys + ["ncols"], writes=tkeys)
        A("sp", lambda e: e.dma_start(out=wqs_d[:, :, kk, :].rearrange("q p n -> p q n"),
                                      in_=tv.rearrange("p (q n) -> p q n", q=4)), reads=tkeys, writes=["wqs"],
          dma_key=("wtmp", kk % 2))

    lw = []
    sink[0] = lw
    load_scaled(win_d, 0, 1536, win_dst, 0)
    load_scaled(wout_d, 8, 1024, wout_dst, 0)
    load_scaled(wq_d, 16, 2048, wq_dst, 0)
    lu = []
    sink[0] = lu

    def setup_load(a):
        ar = a % NSR
        us, uk = ustage[ar]
        vsg, vk = vstage[ar]
        A("sp", lambda e, us=us, a=a: e.dma_start(out=us, in_=pu_d[a * 128:(a + 1) * 128, :]),
          writes=uk, dma_key=("ustage", ar))
        A("sp", lambda e, vsg=vsg, a=a: e.dma_start(out=vsg, in_=pv_d[a * 128:(a + 1) * 128, :]),
          writes=vk, dma_key=("vstage", ar))

    for a in range(NSR - 1):
        setup_load(a)
    for a in range(NA):
        ar = a % NSR
        us, uk = ustage[ar]
        vsg, vk = vstage[ar]
        ub, ubk = utbuf[ar]
        vb, vbk = vbuf[ar]
        if a + NSR - 1 < NA:
            setup_load(a + NSR - 1)
        for kk in range(8):
            bank = 4 + (kk // 4) + 2 * (a % 2)
            A("pe", lambda e, us=us, kk=kk, bank=bank: e.transpose(
                out=ps[bank][:, (kk % 4) * 128:(kk % 4 + 1) * 128], in_=us[:, kk * 128:(kk + 1) * 128],
                identity=ident), reads=uk + ["cm"], excl=[PB(bank)])
        for hf in range(2):
            bank = 4 + hf + 2 * (a % 2)
            A("dve", lambda e, ub=ub, hf=hf, bank=bank: e.tensor_tensor(
                out=ub[:, hf * 512:(hf + 1) * 512].rearrange("p (k n) -> p k n", k=4),
                in0=ps[bank][:].rearrange("p (k n) -> p k n", k=4),
                in1=ncols[:, 16 + hf * 4:20 + hf * 4].unsqueeze(2).to_broadcast([P, 4, 128]), op=ALU.mult),
              reads=["ncols"], writes=ubk, excl=[PB(bank)])
        A("sp", lambda e, ub=ub, a=a: e.dma_start(out=uts_d[a].rearrange("p k n -> p (k n)"), in_=ub),
          reads=ubk, writes=[("uts", a)], dma_key=("utbuf", ar))
        A("act", lambda e, vb=vb, vsg=vsg: e.copy(out=vb, in_=vsg), reads=vk, writes=vbk)
        A("sp", lambda e, vb=vb, a=a: e.dma_start(out=vs_d[a], in_=vb), reads=vbk, writes=[("vs", a)],
          dma_key=("vbuf", ar))

    sink[0] = None
    merge(lu, lw)

    ALLG = [("G", b) for b in range(64)]
    WQK = [("G", b) for b in range(32)]
    wqb = G[:, 0:16384].rearrange("p (q k n) -> p q k n", q=4, k=8)

    def rstd_from(sum_col, scale, cols):
        c0, c1, c2 = cols
        A("dve", lambda e: e.tensor_scalar(out=st[:, c0:c0 + 1], in0=st[:, sum_col:sum_col + 1], scalar1=scale,
                                           scalar2=EPS, op0=ALU.mult, op1=ALU.add),
          reads=[("st", sum_col)], writes=[("st", c0)])
        A("pool", lambda e: e.tensor_tensor(out=st[:, c2:c2 + 1], in0=st[:, c0:c0 + 1], in1=mhalf[:, 0:1],
                                            op=ALU.pow), reads=[("st", c0), "mhalf"], writes=[("st", c2)])

    def transpose8(src16, src_keys, dst_ap, dst_keys):
        pb = ps[3][:].bitcast(BF16)
        for kk in range(8):
            A("pe", lambda e, kk=kk: e.transpose(out=pb[:, kk * 128:(kk + 1) * 128],
                                                 in_=src16[:, kk * 128:(kk + 1) * 128], identity=identb[:]),
              reads=src_keys + ["identb"], excl=[PB(3)])
        A("act", lambda e: e.copy(out=dst_ap, in_=pb.rearrange("p (k n) -> p k n", k=8)), excl=[PB(3)],
          writes=dst_keys)

    def phase_A(g):
        tap = dbg and g == 0
        par = g % 2
        xg = xg2[:, par]
        h2T = h2T2[:, par]
        H2K = [("h2T", par, 0), ("h2T", par, 1)]
        ALLG = [("G", b) for b in range(64)]
        for j in range(2):
            ti = g * 2 + j
            XJ = ("xg", par, j)
            xj = xg[:, j, :]
            A("sp", lambda e, xj=xj, ti=ti: e.dma_start(out=xj, in_=x_d[ti * 128:(ti + 1) * 128, :]),
              writes=[XJ], dma_key=XJ)
            A("act", lambda e, xj=xj: e.activation(out=tb16[:], in_=xj, func=AF.Square, accum_out=st[:, 0:1]),
              reads=[XJ], writes=["tb16", ("st", 0)])
            rstd_from(0, 1.0 / D, (1, 2, 3))
            A("act", lambda e, xj=xj: e.activation(out=tb16[:], in_=xj, func=AF.Copy, scale=st[:, 3:4]),
              reads=[XJ, ("st", 3)], writes=["tb16"])
            transpose8(tb16, ["tb16"], hT[:], ["hT"])
            for kk in range(8):
                for n in range(3):
                    A("pe", lambda e, kk=kk, n=n: e.matmul(ps[n][:], lhsT=hT[:, kk, :],
                                                           rhs=winb[:, kk, n * 512:(n + 1) * 512],
                                                           start=(kk == 0), stop=(kk == 7)),
                      reads=["hT", ("winb", kk)], excl=[PB(n)])
            cur, prv = ti % 2, (ti + 1) % 2
            A("act", lambda e, cur=cur: e.copy(out=p_sb[:, cur, :], in_=ps[0][:]), excl=[PB(0)],
              writes=[("p_sb", cur)])
            A("act", lambda e: e.activation(out=u_sb[:], in_=ps[1][:], func=AF.Gelu), excl=[PB(1)],
              writes=["u_sb"])
            A("act", lambda e: e.activation(out=v_sb[:], in_=ps[2][:], func=AF.Gelu), excl=[PB(2)],
              writes=["v_sb"])
            for gg in range(4):
                first = (ti == 0)
                A("pe", lambda e, gg=gg, cur=cur, first=first: e.matmul(
                    ps[4][:, gg * 128:(gg + 1) * 128], lhsT=p_sb[:, cur, gg * 128:(gg + 1) * 128],
                    rhs=cm[:, (8 + gg) if first else gg, :], start=True, stop=first),
                  reads=[("p_sb", cur), "cm"], excl=[PB(4)])
                if not first:
                    A("pe", lambda e, gg=gg, prv=prv: e.matmul(
                        ps[4][:, gg * 128:(gg + 1) * 128], lhsT=p_sb[:, prv, gg * 128:(gg + 1) * 128],
                        rhs=cm[:, 4 + gg, :], start=False, stop=True),
                      reads=[("p_sb", prv), "cm"], excl=[PB(4)])
            A("act", lambda e: e.copy(out=dTb[:], in_=ps[4][:]), excl=[PB(4)], writes=["dTb"])
            for gg in range(4):
                A("pe", lambda e, gg=gg: e.matmul(ps[5][:, gg * 128:(gg + 1) * 128],
                                                  lhsT=dTb[:, gg * 128:(gg + 1) * 128], rhs=poolwb[:, gg, :],
                                                  start=True, stop=True),
                  reads=["dTb", "poolwb"], excl=[PB(5)])
            v3 = v_sb[:].rearrange("p (h c) -> p h c", h=4)
            tA3 = tmpA[:].rearrange("p (h c) -> p h c", h=4)
            A("dve", lambda e: e.tensor_reduce(out=st[:, 16:20], in_=v3, axis=AX.X, op=ALU.add),
              reads=["v_sb"], writes=[("st", 16)])
            A("act", lambda e: e.activation(out=tmpA[:], in_=v_sb[:], func=AF.Square), reads=["v_sb"],
              writes=["tmpA"])
            A("dve", lambda e: e.tensor_reduce(out=st[:, 20:24], in_=tA3, axis=AX.X, op=ALU.add),
              reads=["tmpA"], writes=[("st", 20)])
            A("dve", lambda e: e.tensor_scalar(out=st[:, 24:28], in0=st[:, 16:20], scalar1=1.0 / 128, scalar2=None,
                                               op0=ALU.mult), reads=[("st", 16)], writes=[("st", 24)])
            A("dve", lambda e: e.tensor_tensor(out=st[:, 28:32], in0=st[:, 24:28], in1=st[:, 24:28], op=ALU.mult),
              reads=[("st", 24)], writes=[("st", 28)])
            A("dve", lambda e: e.scalar_tensor_tensor(out=st[:, 32:36], in0=st[:, 20:24], scalar=1.0 / 128,
                                                      in1=st[:, 28:32], op0=ALU.mult, op1=ALU.subtract),
              reads=[("st", 20), ("st", 28)], writes=[("st", 32)])
            A("dve", lambda e: e.tensor_scalar(out=st[:, 36:40], in0=st[:, 32:36], scalar1=EPS, scalar2=None,
                                               op0=ALU.add), reads=[("st", 32)], writes=[("st", 36)])
            A("pool", lambda e: e.tensor_tensor(out=st[:, 44:48], in0=st[:, 36:40], in1=mhalf[:, 0:4], op=ALU.pow),
              reads=[("st", 36), "mhalf"], writes=[("st", 44)])
            A("dve", lambda e: e.tensor_tensor(out=tA3, in0=v3,
                                               in1=st[:, 24:28].unsqueeze(2).to_broadcast([P, 4, 128]),
                                               op=ALU.subtract), reads=["v_sb", ("st", 24)], writes=["tmpA"])
            A("dve", lambda e: e.tensor_tensor(out=tA3, in0=tA3,
                                               in1=st[:, 44:48].unsqueeze(2).to_broadcast([P, 4, 128]),
                                               op=ALU.mult), reads=[("st", 44)], writes=["tmpA"])
            A("dve", lambda e: e.tensor_tensor(out=tmpA[:], in0=tmpA[:], in1=rows[:, 1, :], op=ALU.mult),
              reads=["rows"], writes=["tmpA"])
            A("dve", lambda e: e.tensor_tensor(out=vnb[:], in0=tmpA[:], in1=rows[:, 2, :], op=ALU.add),
              reads=["rows", "tmpA"], writes=["vnb"])
            for hh in range(4):
                A("pe", lambda e, hh=hh: e.matmul(ps[4][:, hh * 128:(hh + 1) * 128], lhsT=sguwb[:, hh, :],
                                                  rhs=vnb[:, hh * 128:(hh + 1) * 128], start=True, stop=True),
                  reads=["sguwb", "vnb"], excl=[PB(4)])
            tB3 = tmpB[:].rearrange("p (h c) -> p h c", h=4)
            A("dve", lambda e: e.tensor_tensor(out=tB3, in0=ps[4][:].rearrange("p (h c) -> p h c", h=4),
                                               in1=sgub[:].unsqueeze(2).to_broadcast([P, 4, 128]), op=ALU.add),
              reads=["sgub"], writes=["tmpB"], excl=[PB(4)])
            A("dve", lambda e: e.tensor_tensor(out=tmpB[:], in0=tmpB[:], in1=u_sb[:], op=ALU.mult),
              reads=["u_sb"], writes=["tmpB"])
            if tap and j == 0:
                A("dve", lambda e: e.tensor_copy(out=tmpA[:], in_=ps[5][:]), excl=[PB(5)], writes=["tmpA"])
                A("sp", lambda e: e.dma_start(out=dbg_d["d_aout"], in_=tmpA[:]), reads=["tmpA"], dma_key="dbg1")
                A("sp", lambda e: e.dma_start(out=dbg_d["d_bout"], in_=tmpB[:]), reads=["tmpB"], dma_key="dbg2")
            A("act", lambda e: e.activation(out=tb16[:, 0:512], in_=ps[5][:], func=AF.Square,
                                            accum_out=st[:, 4:5]), excl=[PB(5)], writes=["tb16", ("st", 4)])
            A("act", lambda e: e.activation(out=tb16[:, 512:1024], in_=tmpB[:], func=AF.Square,
                                            accum_out=st[:, 5:6]), reads=["tmpB"], writes=["tb16", ("st", 5)])
            A("dve", lambda e: e.tensor_scalar(out=st[:, 6:8], in0=st[:, 4:6], scalar1=1.0 / 512, scalar2=EPS,
                                               op0=ALU.mult, op1=ALU.add), reads=[("st", 4), ("st", 5)],
              writes=[("st", 6)])
            A("pool", lambda e: e.tensor_tensor(out=st[:, 10:12], in0=st[:, 6:8], in1=mhalf[:, 0:2], op=ALU.pow),
              reads=[("st", 6), "mhalf"], writes=[("st", 10)])
            A("dve", lambda e: e.tensor_scalar(out=tb16[:, 0:512], in0=ps[5][:], scalar1=st[:, 10:11], scalar2=None,
                                               op0=ALU.mult), reads=[("st", 10)], writes=["tb16"], excl=[PB(5)])
            A("dve", lambda e: e.tensor_scalar(out=tb16[:, 512:1024], in0=tmpB[:], scalar1=st[:, 11:12],
                                               scalar2=None, op0=ALU.mult), reads=[("st", 10), "tmpB"],
              writes=["tb16"])
            transpose8(tb16, ["tb16"], hT[:], ["hT"])
            for kk in range(8):
                for n in range(2):
                    A("pe", lambda e, kk=kk, n=n: e.matmul(ps[n][:], lhsT=hT[:, kk, :],
                                                           rhs=woutb[:, kk, n * 512:(n + 1) * 512],
                                                           start=(kk == 0), stop=(kk == 7)),
                      reads=["hT", ("woutb", kk)], excl=[PB(n)])
            for n in range(2):
                A("dve", lambda e, n=n, j=j: e.tensor_tensor(out=xg[:, j, n * 512:(n + 1) * 512], in0=ps[n][:],
                                                             in1=xg[:, j, n * 512:(n + 1) * 512], op=ALU.add),
                  writes=[XJ], excl=[PB(n)])
            if tap:
                A("sp", lambda e, xj=xj, j=j: e.dma_start(out=dbg_d["d_x2"][j * 128:(j + 1) * 128, :], in_=xj),
                  reads=[XJ], dma_key=("dbgx", j))
            A("act", lambda e, xj=xj: e.activation(out=tb16[:], in_=xj, func=AF.Square, accum_out=st[:, 12:13]),
              reads=[XJ], writes=["tb16", ("st", 12)])
            rstd_from(12, 1.0 / D, (13, 14, 15))
            A("act", lambda e, xj=xj: e.activation(out=tb16[:], in_=xj, func=AF.Copy, scale=st[:, 15:16]),
              reads=[XJ, ("st", 15)], writes=["tb16"])
            transpose8(tb16, ["tb16"], h2T[:, :, j * 128:(j + 1) * 128], [("h2T", par, j)])
        if tap:
            A("act", lambda e: e.copy(out=arB1[:], in_=h2T[:].rearrange("p k t -> p (k t)")), reads=H2K,
              writes=[("arB1", s) for s in range(4)])
            A("sp", lambda e: e.dma_start(out=dbg_d["d_h2T"], in_=arB1[:]), reads=[("arB1", s) for s in range(4)],
              dma_key="dbg3")

    def phase_B(g):
        tap = dbg and g == 0
        par = g % 2
        xg = xg2[:, par]
        h2T = h2T2[:, par]
        H2K = [("h2T", par, 0), ("h2T", par, 1)]
        ALLG = [("G", b) for b in range(64)]
        A("dve", lambda e: e.memset(bar[:, 0:1], 0.0), reads=BTMPK, writes=BTMPK + BKEYS)
        AB1 = [("arB1", s) for s in range(4)]
        AB2 = [("arB2", s) for s in range(4)]
        for qq in range(4):
            qs = qq % 2
            qTq = arB2[:, qs * 1024:(qs + 1) * 1024].rearrange("p (c t) -> p c t", c=4)
            s_q = arB1[:, qs * 1024:(qs + 1) * 1024].rearrange("p (j c k) -> p j c k", j=2, c=4)
            for cc in range(4):
                c = qq * 4 + cc
                for kk in range(8):
                    A("pe", lambda e, cc=cc, c=c, kk=kk, qq=qq: e.matmul(
                        ps[6 + cc // 2][:, (cc % 2) * 256:(cc % 2 + 1) * 256],
                        lhsT=wqb[:, qq, kk, cc * 128:(cc + 1) * 128], rhs=h2T[:, kk, :], start=(kk == 0),
                        stop=(kk == 7)),
                      reads=[("G", b) for b in range(qq * 8, qq * 8 + 8)] + H2K, excl=[PB(6 + cc // 2)])
            for b2 in range(2):
                A("act", lambda e, b2=b2, qs=qs: e.copy(
                    out=arB2[:, qs * 1024 + b2 * 512: qs * 1024 + (b2 + 1) * 512], in_=ps[6 + b2][:]),
                  excl=[PB(6 + b2)], writes=[("arB2", qs * 2 + b2)])
            for j in range(2):
                for cc in range(4):
                    c = qq * 4 + cc
                    A("pe", lambda e, cc=cc, c=c, j=j, qTq=qTq: e.matmul(
                        ps[6 + j][:, cc * 128:(cc + 1) * 128], lhsT=qTq[:, cc, j * 128:(j + 1) * 128],
                        rhs=keysT[:, c % 2, :], start=True, stop=True),
                      reads=[("arB2", qs * 2 + cc // 2), "keysT"], excl=[PB(6 + j)])
                A("act", lambda e, j=j, qs=qs: e.copy(
                    out=arB1[:, qs * 1024 + j * 512: qs * 1024 + (j + 1) * 512], in_=ps[6 + j][:]),
                  excl=[PB(6 + j)], writes=[("arB1", qs * 2 + j)])
            for j in range(2):
                for cp in range(2):
                    steps = [[], [], [], [], []]
                    for cc in (2 * cp, 2 * cp + 1):
                        c = qq * 4 + cc
                        hh, pp = c // 2, c % 2
                        sk = [("arB1", qs * 2 + j)]
                        src = s_q[:, j, cc, :]
                        tk = [("top", j, c)]
                        ik = [("idx", j, c)]
                        rr = c % 2
                        steps[0].append(("dve", lambda e, src=src, j=j, hh=hh, pp=pp: e.max(
                            out=top[:, j, hh, pp, 0:8], in_=src), dict(reads=sk, writes=tk)))
                        steps[1].append(("dve", lambda e, src=src, j=j, hh=hh, pp=pp: e.max_index(
                            out=idx[:, j, hh, pp, 0:8], in_max=top[:, j, hh, pp, 0:8], in_values=src),
                            dict(reads=sk + tk, writes=ik)))
                        steps[2].append(("dve", lambda e, src=src, j=j, hh=hh, pp=pp, rr=rr: e.match_replace(
                            out=s2r[:, rr, :], in_to_replace=top[:, j, hh, pp, 0:8], in_values=src, imm_value=NEG),
                            dict(reads=sk + tk, writes=[("s2r", rr)])))
                        steps[3].append(("dve", lambda e, j=j, hh=hh, pp=pp, rr=rr: e.max(
                            out=top[:, j, hh, pp, 8:16], in_=s2r[:, rr, :]), dict(reads=[("s2r", rr)], writes=tk)))
                        steps[4].append(("dve", lambda e, j=j, hh=hh, pp=pp, rr=rr: e.max_index(
                            out=idx[:, j, hh, pp, 8:16], in_max=top[:, j, hh, pp, 8:16], in_values=s2r[:, rr, :]),
                            dict(reads=[("s2r", rr)] + tk, writes=ik)))
                    for stp in steps:
                        for (en_, fn_, kw_) in stp:
                            A(en_, fn_, **kw_)
        if tap:
            A("sp", lambda e: e.dma_start(out=dbg_d["d_top"], in_=top[:].rearrange("p j h q k -> p (j h q k)")),
              reads=[("top", j, c) for j in range(2) for c in range(16)], dma_key="dbg4")

        for j in range(2):
            TK = [("top", j, c) for c in range(16)]
            IK = [("idx", j, c) for c in range(16)]
            cand = arB1[:].rearrange("p (h i k) -> p h i k", h=8, i=16)
            A("dve", lambda e, j=j, cand=cand: e.tensor_tensor(
                out=cand, in0=top[:, j, :, 0, :].unsqueeze(3).to_broadcast([P, 8, 16, 16]),
                in1=top[:, j, :, 1, :].unsqueeze(2).to_broadcast([P, 8, 16, 16]), op=ALU.add),
              reads=TK, writes=AB1)
            cand_f = arB1[:].rearrange("p (h n) -> p h n", h=8)
            for hp in range(4):
                steps = [[], [], [], [], []]
                for hh in (2 * hp, 2 * hp + 1):
                    rr = hh % 2
                    src = cand_f[:, hh, :]
                    steps[0].append((lambda e, src=src, hh=hh: e.max(out=cv[:, hh, 0:8], in_=src),
                                     dict(reads=AB1, writes=[("cv", hh)])))
                    steps[1].append((lambda e, src=src, hh=hh: e.max_index(out=ci[:, hh, 0:8], in_max=cv[:, hh, 0:8],
                                                                          in_values=src),
                                     dict(reads=AB1 + [("cv", hh)], writes=[("ci", hh)])))
                    steps[2].append((lambda e, src=src, hh=hh, rr=rr: e.match_replace(
                        out=cand2[:, rr, :], in_to_replace=cv[:, hh, 0:8], in_values=src, imm_value=NEG),
                        dict(reads=AB1 + [("cv", hh)], writes=[("cand2", rr)])))
                    steps[3].append((lambda e, hh=hh, rr=rr: e.max(out=cv[:, hh, 8:16], in_=cand2[:, rr, :]),
                                     dict(reads=[("cand2", rr)], writes=[("cv", hh)])))
                    steps[4].append((lambda e, hh=hh, rr=rr: e.max_index(out=ci[:, hh, 8:16], in_max=cv[:, hh, 8:16],
                                                                        in_values=cand2[:, rr, :]),
                                     dict(reads=[("cand2", rr), ("cv", hh)], writes=[("ci", hh)])))
                for stp in steps:
                    for (fn_, kw_) in stp:
                        A("dve", fn_, **kw_)
            CVK = [("cv", h) for h in range(8)]
            CIK = [("ci", h) for h in range(8)]
            if tap and j == 0:
                A("sp", lambda e: e.dma_start(out=dbg_d["d_cv"], in_=cv[:].rearrange("p h k -> p (h k)")),
                  reads=CVK, dma_key="dbg5")
            A("dve", lambda e: e.tensor_tensor(out=ge[:], in0=cv[:], in1=cv[:, :, 0:1].to_broadcast([P, 8, 16]),
                                               op=ALU.subtract), reads=CVK, writes=["ge"])
            A("act", lambda e: e.activation(out=ge[:], in_=ge[:], func=AF.Exp), writes=["ge"])
            A("dve", lambda e: e.tensor_reduce(out=gs[:, 0:8], in_=ge[:], axis=AX.X, op=ALU.add), reads=["ge"],
              writes=["gs"])
            A("dve", lambda e: e.reciprocal(out=gs[:, 8:16], in_=gs[:, 0:8]), writes=["gs"])
            A("dve", lambda e: e.tensor_tensor(out=gate[:], in0=ge[:],
                                               in1=gs[:, 8:16].unsqueeze(2).to_broadcast([P, 8, 16]), op=ALU.mult),
              reads=["ge", "gs"], writes=["gate"])
            ci_f = ci[:].rearrange("p h k -> p (h k)")
            A("dve", lambda e: e.tensor_single_scalar(out=selu[:, 0, :], in_=ci_f, scalar=4,
                                                      op=ALU.logical_shift_right), reads=CIK, writes=["selu"])
            A("dve", lambda e: e.tensor_single_scalar(out=selu[:, 1, :], in_=ci_f, scalar=15, op=ALU.bitwise_and),
              reads=CIK, writes=["selu"])
            A("dve", lambda e: e.tensor_copy(out=selF[:], in_=selu[:]), reads=["selu"], writes=["selF"])
            A("dve", lambda e, j=j: e.tensor_copy(out=idxF[:], in_=idx[:, j]), reads=IK, writes=["idxF"])
            ohi = arB1[:].rearrange("p (n i) -> p n i", i=16)
            ohi4 = arB1[:].rearrange("p (h k i) -> p h k i", h=8, k=16)
            for pp in range(2):
                A("dve", lambda e, pp=pp: e.tensor_tensor(
                    out=ohi, in0=selF[:, pp, :].unsqueeze(2).to_broadcast([P, 128, 16]),
                    in1=iota[:, 0:16].unsqueeze(1).to_broadcast([P, 128, 16]), op=ALU.is_equal),
                  reads=["selF", "iota"], writes=AB1)
                A("dve", lambda e, pp=pp: e.tensor_tensor(
                    out=ohi4, in0=ohi4, in1=idxF[:, :, pp, :].unsqueeze(2).to_broadcast([P, 8, 16, 16]),
                    op=ALU.mult), reads=["idxF"], writes=AB1)
                A("dve", lambda e, pp=pp: e.tensor_reduce(out=abF[:, pp, :], in_=ohi, axis=AX.X, op=ALU.add),
                  reads=AB1, writes=[("abF", pp)])
            srcs = [abF[:, 0, :], abF[:, 1, :], gate[:].rearrange("p h k -> p (h k)")]
            for q3 in range(3):
                A("pe", lambda e, q3=q3: e.transpose(out=ps[6][:, q3 * 128:(q3 + 1) * 128], in_=srcs[q3],
                                                     identity=ident),
                  reads=[("abF", 0), ("abF", 1), "gate", "cm"], excl=[PB(6)])
            A("act", lambda e, j=j: e.copy(out=abg[:, :, j * 128:(j + 1) * 128],
                                           in_=ps[6][:, 0:384].rearrange("p (q t) -> p q t", q=3)),
              excl=[PB(6)], writes=[("abg", j)])
        if tap:
            A("sp", lambda e: e.dma_start(out=dbg_d["d_abg"], in_=abg[:].rearrange("p q t -> p (q t)")),
              reads=[("abg", 0), ("abg", 1)], dma_key="dbg6")

        A("dve", lambda e: e.memset(bar[:, 1:2], 0.0), writes=BTMPK + BKEYS)
    def phase_C(g):
        tap = dbg and g == 0
        par = g % 2
        xg = xg2[:, par]
        h2T = h2T2[:, par]
        H2K = [("h2T", par, 0), ("h2T", par, 1)]
        ALLG = [("G", b) for b in range(64)]
        NB = TG // CB
        PREP_EVERY = 3

        def c_prep(tb):
            if tb % PREP_EVERY:
                return
            t0 = tb * CB
            r = (tb // PREP_EVERY) % RR
            jt = t0 // 128
            src = abg[:, :, t0:t0 + CB].rearrange("p q t -> p t q").unsqueeze(3).to_broadcast([P, CB, 3, 128])
            A("act", lambda e: e.copy(out=rep[:, r], in_=src), reads=[("abg", jt)], writes=[("rep", r)])

        def c_compare(tb):
            r = tb % CR
            t0 = tb * CB
            jt = t0 // 128
            if tb % PREP_EVERY == 0:
                rr = (tb // PREP_EVERY) % RR
                A("dve", lambda e: e.tensor_tensor(out=OH[:, r], in0=iota_rep[:], in1=rep[:, rr, :, 0:2, :],
                                                   op=ALU.is_equal),
                  reads=["iota_rep", ("rep", rr)], writes=[("OH", r)])
            else:
                sel_ab = abg[:, 0:2, t0:t0 + CB].rearrange("p q t -> p t q").unsqueeze(3).to_broadcast(
                    [P, CB, 2, 128])
                A("dve", lambda e: e.tensor_tensor(out=OH[:, r], in0=iota_rep[:], in1=sel_ab, op=ALU.is_equal),
                  reads=["iota_rep", ("abg", jt)], writes=[("OH", r)])

        def c_scale(tb):
            r = tb % CR
            t0 = tb * CB
            jt = t0 // 128
            if tb % PREP_EVERY == 0:
                rr = (tb // PREP_EVERY) % RR
                A("dve", lambda e: e.tensor_tensor(out=OH[:, r, :, 0, :], in0=OH[:, r, :, 0, :],
                                                   in1=rep[:, rr, :, 2, :], op=ALU.mult),
                  reads=[("rep", rr)], writes=[("OH", r)])
            else:
                A("pool", lambda e: e.tensor_tensor(
                    out=OH[:, r, :, 0, :], in0=OH[:, r, :, 0, :],
                    in1=abg[:, 2, t0:t0 + CB].unsqueeze(2).to_broadcast([P, CB, 128]), op=ALU.mult),
                  reads=[("abg", jt)], writes=[("OH", r)])

        def c_mm(tb):
            t0 = tb * CB
            r = tb % CR
            for tt in range(CB):
                t = t0 + tt
                bank = 6 + (t // 4) % 2
                A("pe", lambda e, t=t, tt=tt, bank=bank: e.matmul(
                    ps[bank][:, (t % 4) * 128:(t % 4 + 1) * 128], lhsT=OH[:, r, tt, 1, :], rhs=OH[:, r, tt, 0, :],
                    start=True, stop=True),
                  reads=[("OH", r)], excl=[PB(bank)])
                if t % 4 == 3:
                    blk = t // 4
                    A("act", lambda e, blk=blk, bank=bank: e.copy(out=G[:, blk * 512:(blk + 1) * 512],
                                                                  in_=ps[bank][:]),
                      excl=[PB(bank)], writes=[("G", blk)])

        for tb in range(min(PREP_EVERY * RR, NB)):
            c_prep(tb)
        c_compare(0)
        for tb in range(NB):
            if tb + 1 < NB:
                c_compare(tb + 1)
            c_scale(tb)
            c_mm(tb)
            if tb + PREP_EVERY * RR < NB:
                c_prep(tb + PREP_EVERY * RR)
        if tap:
            A("act", lambda e: e.copy(out=arB1[:, 0:1024], in_=G[:, 0:1024]), reads=ALLG,
              writes=[("arB1", 0), ("arB1", 1)])
            A("sp", lambda e: e.dma_start(out=dbg_d["d_G"], in_=arB1[:, 0:1024]),
              reads=[("arB1", 0), ("arB1", 1)], dma_key="dbg7")

    def phase_D(g):
        tap = dbg and g == 0
        par = g % 2
        xg = xg2[:, par]
        h2T = h2T2[:, par]
        H2K = [("h2T", par, 0), ("h2T", par, 1)]
        ALLG = [("G", b) for b in range(64)]
        NRG = 5
        ut_v = [arB1[:, s * 512:(s + 1) * 512].bitcast(BF16).rearrange("p (k n) -> p k n", k=8) for s in range(4)]
        ut_v.append(utx[:].bitcast(BF16).rearrange("p (k n) -> p k n", k=8))
        vt_v = [arB2[:, s * 512:(s + 1) * 512].bitcast(BF16) for s in range(4)]
        vt_v.append(vtx[:].bitcast(BF16))
        UK = [("arB1", s) for s in range(4)] + ["utx"]
        VK = [("arB2", s) for s in range(4)] + ["vtx"]

        def load_u(a):
            s = a % NRG
            A("sp", lambda e, a=a, s=s: e.dma_start(out=ut_v[s], in_=uts_d[a]), reads=[("uts", a)],
              writes=[UK[s]], dma_key=("ut", s))

        def load_v(a):
            s = a % NRG
            A("sp", lambda e, a=a, s=s: e.dma_start(out=vt_v[s], in_=vs_d[a]), reads=[("vs", a)],
              writes=[VK[s]], dma_key=("vt", s))

        def d_first(a):
            s = a % NRG
            bank = 4 + a % 2
            for kk in range(8):
                A("pe", lambda e, s=s, kk=kk, bank=bank: e.matmul(ps[bank][:, 0:TG], lhsT=ut_v[s][:, kk, :],
                                                                  rhs=h2T[:, kk, :], start=(kk == 0),
                                                                  stop=(kk == 7)),
                  reads=[UK[s]] + H2K, excl=[PB(bank)])

        for a in range(min(NRG, na_main)):
            load_u(a)
            if a < NRG - 1:
                load_v(a)
        d_first(0)
        for a in range(na_main):
            if a + NRG - 1 < na_main:
                load_v(a + NRG - 1)
            if a + NRG < na_main:
                load_u(a + NRG)
            if a + 1 < na_main:
                d_first(a + 1)
            s = a % NRG
            r2 = a % 2
            bank = 4 + r2
            A("act", lambda e, r2=r2, bank=bank: e.activation(out=gl[:, r2, :], in_=ps[bank][:, 0:TG],
                                                              func=AF.Gelu),
              excl=[PB(bank)], writes=[("gl", r2)])
            A("dve", lambda e, r2=r2, a=a: e.tensor_tensor(out=ga[:, r2, :], in0=gl[:, r2, :],
                                                           in1=G[:, a:TG * 128:128], op=ALU.mult),
              reads=[("gl", r2)] + ALLG, writes=[("ga", r2)])
            for tt in range(2):
                for n in range(2):
                    A("pe", lambda e, s=s, r2=r2, tt=tt, n=n, a=a: e.matmul(
                        ps[tt * 2 + n][:], lhsT=ga[:, r2, tt * 128:(tt + 1) * 128],
                        rhs=vt_v[s][:, n * 512:(n + 1) * 512], start=(a == 0), stop=(a == na_main - 1)),
                      reads=[("ga", r2), VK[s]], excl=[PB(tt * 2 + n)])

    def phase_E(g):
        tap = dbg and g == 0
        par = g % 2
        xg = xg2[:, par]
        h2T = h2T2[:, par]
        H2K = [("h2T", par, 0), ("h2T", par, 1)]
        ALLG = [("G", b) for b in range(64)]
        for j in range(2):
            ti = g * 2 + j
            XJ = ("xg", par, j)
            x3 = atmp[:, 2 * j:2 * j + 2, :].rearrange("p a n -> p (a n)")
            XK = ["u_sb", "v_sb"] if j == 0 else ["tmpA", "tmpB"]
            c0 = 48 + 4 * j
            for n in range(2):
                A("dve", lambda e, n=n, j=j, x3=x3: e.tensor_tensor(out=x3[:, n * 512:(n + 1) * 512],
                                                                    in0=ps[j * 2 + n][:],
                                                                    in1=xg[:, j, n * 512:(n + 1) * 512], op=ALU.add),
                  reads=[XJ], writes=[XK[n]], excl=[PB(j * 2 + n)])
            A("act", lambda e, x3=x3, c0=c0: e.activation(out=tb16[:], in_=x3, func=AF.Square,
                                                          accum_out=st[:, c0:c0 + 1]),
              reads=XK, writes=["tb16", ("st", c0)])
            rstd_from(c0, 1.0 / D, (c0 + 1, c0 + 2, c0 + 3))
            A("dve", lambda e, x3=x3, c0=c0: e.scalar_tensor_tensor(out=x3, in0=x3, scalar=st[:, c0 + 3:c0 + 4],
                                                                    in1=nfin[:], op0=ALU.mult, op1=ALU.mult),
              reads=[("st", c0 + 3), "nfin"], writes=XK)
            A("sp", lambda e, x3=x3, ti=ti: e.dma_start(out=out_d[ti * 128:(ti + 1) * 128, :], in_=x3),
              reads=XK, dma_key=("ost", j))

    def load_wq():
        for qq in range(4):
            A("sp", lambda e, qq=qq: e.dma_start(out=wqb[:, qq], in_=wqs_d[qq]), reads=["wqs"],
              writes=[("G", b) for b in range(qq * 8, qq * 8 + 8)], dma_key=("wqb", qq))

    def merged(fns_main, fn_side):
        lm, ls = [], []
        cuts = []
        sink[0] = lm
        for f in fns_main:
            f()
            cuts.append(len(lm))
        sink[0] = ls
        if fn_side is not None:
            fn_side()
        sink[0] = None
        if ls:
            cut = cuts[0] + (4 * (len(lm) - cuts[0])) // 10 if len(cuts) > 1 else len(lm)
            merge(lm[:cut], ls, lead=46)
            for a_, kw_ in lm[cut:]:
                pg.add(*a_, **kw_)
        else:
            for a_, kw_ in lm:
                pg.add(*a_, **kw_)

    load_wq()
    phase_A(0)
    merged([lambda: phase_B(0), lambda: phase_C(0)], (lambda: phase_A(1)) if NGRP > 1 else None)
    for g in range(NGRP):
        phase_D(g)
        if g + 1 < NGRP:
            load_wq()
        phase_E(g)
        if g + 1 < NGRP:
            merged([lambda g=g: phase_B(g + 1), lambda g=g: phase_C(g + 1)],
                   (lambda g=g: phase_A(g + 2)) if g + 2 < NGRP else None)

    pg.emit()
    return nc


def _host_consts():
    cm = np.zeros((P, 14, 128), np.float32)
    s = np.arange(128)[:, None]
    t = np.arange(128)[None, :]
    for g, win in enumerate((2, 4, 8, 16)):
        band = ((t - s) >= 0) & ((t - s) <= win - 1)
        cm[:, g, :] = band * (1.0 / win) - (s == t)
        cm[:, 4 + g, :] = ((t - (s - 128)) <= win - 1) * (1.0 / win)
        cnt = np.minimum(t + 1, win).astype(np.float64)
        cm[:, 8 + g, :] = band / cnt - (s == t)
    cm[:, 12, :] = (s <= t)
    cm[:, 13, :] = (s == t)
    return cm


def _prep_shared(inp):
    f = lambda k: np.ascontiguousarray(inp[k], dtype=np.float32)
    col = lambda v: np.ascontiguousarray(v.reshape(8, 128).T)
    ncols = np.concatenate([col(f("norm_mix")[0]),
                            col(np.concatenate([f("out_norm_pool")[0], f("out_norm_sgu")[0]])),
                            col(f("norm_ffn")[0])], axis=1)
    rows512 = np.stack([f("pool_scale")[0].reshape(512), f("sgu_ln_g")[0].reshape(512),
                        f("sgu_ln_b")[0].reshape(512)])
    shared = {
        "w_in": f("w_in")[0], "w_out": f("w_out")[0], "wq": f("peer_wq")[0],
        "peer_u": f("peer_u")[0], "peer_v": f("peer_v")[0],
        "ncols": np.ascontiguousarray(ncols),
        "rows512": np.ascontiguousarray(np.broadcast_to(rows512[None], (P, 3, 512))),
        "nfin": np.ascontiguousarray(np.broadcast_to(f("norm_final")[None, :], (P, D))),
        "poolw": np.ascontiguousarray(f("pool_w")[0].transpose(1, 0, 2)),
        "sguwT": np.ascontiguousarray(f("sgu_w")[0].transpose(2, 0, 1)),
        "sgub": np.ascontiguousarray(f("sgu_b")[0].T),
        "keysT": np.ascontiguousarray(f("peer_keys")[0].transpose(2, 0, 1)),
        "cm": _host_consts(),
    }
    return shared


def kernel(**inputs):
    x = np.ascontiguousarray(inputs["x"], dtype=np.float32)
    B, S, _ = x.shape
    shared = _prep_shared(inputs)
    nc = build_nc(S)
    in_maps = []
    for b in range(B):
        m = dict(shared)
        m["x"] = x[b]
        in_maps.append(m)
    res = run_bass_kernel_spmd(nc, in_maps, core_ids=list(range(B)))
    return np.stack([np.asarray(r["out"], dtype=np.float32) for r in res.results], axis=0)
```

```python
import numpy as np
import concourse.bass as bass
import concourse.mybir as mybir
from concourse.bass_utils import run_bass_kernel_spmd

F32 = mybir.dt.float32
BF16 = mybir.dt.bfloat16
U32 = mybir.dt.uint32
ALU = mybir.AluOpType
AF = mybir.ActivationFunctionType
AX = mybir.AxisListType

ENGS = ("pe", "act", "dve", "pool", "sp")
P = 128
D = 1024
TG = 256
NA = 128
EPS = 1e-6
NEG = -1.0e30


class _Op:
    __slots__ = ("eng", "fn", "deps", "waits", "seq", "dma_sem", "dma_val", "needed", "nofuse")


class Prog:
    def __init__(self, nc):
        self.nc = nc
        self.glob = []
        self.last_writer = {}
        self.readers = {}
        self.eng_sem = {}
        self.dma_sems = {}
        for e in ("pe", "act", "dve", "pool"):
            self.eng_sem[e] = nc.alloc_semaphore("sem_" + e)

    def _dma_sem(self, key):
        if key not in self.dma_sems:
            self.dma_sems[key] = [self.nc.alloc_semaphore("dsem_%d" % len(self.dma_sems)), 0]
        return self.dma_sems[key]

    def add(self, eng, fn, reads=(), writes=(), dma_key=None, excl=(), nofuse=False):
        op = _Op()
        op.nofuse = nofuse
        op.eng = eng
        op.fn = fn
        op.dma_sem = None
        op.dma_val = 0
        op.seq = 0
        op.needed = False
        op.waits = []
        deps = []
        writes = tuple(writes) + tuple(excl)
        for r in reads:
            w = self.last_writer.get(r)
            if w is not None:
                deps.append(w)
        for r in writes:
            w = self.last_writer.get(r)
            if w is not None:
                deps.append(w)
            deps.extend(self.readers.get(r, ()))
        op.deps = deps
        if dma_key is not None:
            s = self._dma_sem(dma_key)
            s[1] += 16
            op.dma_sem = s[0]
            op.dma_val = s[1]
        for r in reads:
            if r in writes:
                continue
            self.readers.setdefault(r, []).append(op)
        for r in writes:
            self.last_writer[r] = op
            self.readers[r] = []
        self.glob.append(op)
        return op

    def finalize(self):
        last = {}
        for op in self.glob:
            last[op.eng] = op
            for d in op.deps:
                if d.dma_sem is None and d.eng == "pe" and op.eng == "pe":
                    continue
                d.needed = True
        for e, op in last.items():
            op.needed = True
        count = {e: 0 for e in ENGS}
        waited = {e: {} for e in ENGS}
        self.ops = {e: [] for e in ENGS}
        for op in self.glob:
            waits = {}
            for d in op.deps:
                if d.dma_sem is not None:
                    key, sem, val = ("d", id(d.dma_sem)), d.dma_sem, d.dma_val
                else:
                    if d.eng == "pe" and op.eng == "pe":
                        continue
                    key, sem, val = ("e", d.eng), self.eng_sem[d.eng], d.seq
                if waited[op.eng].get(key, 0) >= val:
                    continue
                waited[op.eng][key] = val
                waits[key] = (sem, val)
            op.waits = list(waits.values())
            if op.dma_sem is None and op.needed:
                count[op.eng] += 1
                op.seq = count[op.eng]
            self.ops[op.eng].append(op)
        self.count = count

    def emit(self):
        nc = self.nc
        prog = self
        self.finalize()

        def run(engname, e):
            for op in prog.ops[engname]:
                fuse = bool(op.waits) and not op.nofuse and op.dma_sem is None
                sep = op.waits[:-1] if fuse else op.waits
                for (sem, val) in sep:
                    e.wait_ge(sem, val)
                n0 = nc.n_instructions()
                ins = op.fn(e)
                if fuse:
                    assert nc.n_instructions() - n0 == 1, ("multi-instruction op needs nofuse=True", engname)
                    ins._wait_ge(*op.waits[-1])
                if op.dma_sem is not None:
                    ins.then_inc(op.dma_sem, 16)
                elif op.needed:
                    ins.then_inc(prog.eng_sem[engname], 1)
            if engname == "sp":
                for en in ("pe", "act", "dve", "pool"):
                    if prog.count[en]:
                        e.wait_ge(prog.eng_sem[en], prog.count[en])
                for key, (sem, cnt) in prog.dma_sems.items():
                    if cnt:
                        e.wait_ge(sem, cnt)

        with nc.Block() as block:
            @block.tensor
            def _(e):
                run("pe", e)

            @block.scalar
            def _(e):
                run("act", e)

            @block.vector
            def _(e):
                run("dve", e)

            @block.gpsimd
            def _(e):
                run("pool", e)

            @block.sync
            def _(e):
                run("sp", e)


def gkeys(byte_off, nbytes):
    return [("G", b) for b in range(byte_off // 1024, (byte_off + nbytes - 1) // 1024 + 1)]


def build_nc(NT, dbg=False, n_groups=None, na_main=NA):
    nc = bass.Bass("TRN2", target_bir_lowering=False)
    NGRP = NT // TG if n_groups is None else n_groups

    def din(name, shape, dt=F32):
        return nc.dram_tensor(name, list(shape), dt, kind="ExternalInput").ap()

    x_d = din("x", [NT, D])
    win_d = din("w_in", [D, 1536])
    wout_d = din("w_out", [D, D])
    wq_d = din("wq", [D, 2048])
    pu_d = din("peer_u", [16384, D])
    pv_d = din("peer_v", [16384, D])
    ncols_d = din("ncols", [P, 24])
    rows_d = din("rows512", [P, 3, 512])
    nfin_d = din("nfin", [P, D])
    poolw_d = din("poolw", [P, 4, 128])
    sguwT_d = din("sguwT", [P, 4, 128])
    sgub_d = din("sgub", [P, 4])
    keysT_d = din("keysT", [P, 2, 128])
    cm_d = din("cm", [P, 14, 128])
    out_d = nc.dram_tensor("out", [NT, D], F32, kind="ExternalOutput").ap()

    uts_d = nc.dram_tensor("uts", [NA, P, 8, 128], BF16).ap()
    vs_d = nc.dram_tensor("vs", [NA, P, D], BF16).ap()
    wqs_d = nc.dram_tensor("wqs", [4, P, 8, 512], BF16).ap()

    dbg_d = {}
    if dbg:
        for nm, shp in (("d_x2", [TG, D]), ("d_h2T", [P, 8 * TG]), ("d_abg", [P, 3 * TG]),
                        ("d_top", [P, 2 * 256]), ("d_cv", [P, 128]), ("d_G", [P, 8 * 128]),
                        ("d_aout", [P, 512]), ("d_bout", [P, 512])):
            dbg_d[nm] = nc.dram_tensor(nm, shp, F32, kind="ExternalOutput").ap()

    def sb(name, shape, dt=F32):
        return nc.alloc_sbuf_tensor("s_" + name, list(shape), dt)

    winb = sb("winb", [P, 8, 1536], BF16)
    woutb = sb("woutb", [P, 8, 1024], BF16)
    G = sb("G", [P, TG * 128], BF16)
    cm = sb("cm", [P, 14, 128])
    rows = sb("rows", [P, 3, 512])
    nfin = sb("nfin", [P, D])
    ncols = sb("ncols", [P, 24])
    poolwb = sb("poolwb", [P, 4, 128], BF16)
    sguwb = sb("sguwb", [P, 4, 128], BF16)
    sgub = sb("sgub", [P, 4])
    keysT = sb("keysT", [P, 2, 128])
    identb = sb("identb", [P, 128], BF16)
    iota = sb("iota", [P, 128])
    xg2 = sb("xg", [P, 2, 2, D])
    h2T2 = sb("h2T", [P, 2, 8, TG], BF16)
    tb16 = sb("tb16", [P, D], BF16)
    hT = sb("hT", [P, 8, 128], BF16)
    p_sb = sb("p_sb", [P, 2, 512])
    atmp = sb("atmp", [P, 4, 512])
    u_sb = atmp[:, 0, :]
    v_sb = atmp[:, 1, :]
    tmpA = atmp[:, 2, :]
    tmpB = atmp[:, 3, :]
    dTb = sb("dTb", [P, 512], BF16)
    vnb = sb("vnb", [P, 512], BF16)
    st = sb("st", [P, 64])
    utx = sb("utx", [P, 512])
    vtx = sb("vtx", [P, 512])
    arB1 = sb("arB1", [P, 2048])
    arB2 = sb("arB2", [P, 2048])
    abg = sb("abg", [P, 3, TG])
    CB = 4
    CR = 3
    OH = sb("OH", [P, CR, CB, 2, 128], BF16)
    RR = 2
    rep = sb("rep", [P, RR, CB, 3, 128], BF16)
    iota_rep = sb("iota_rep", [P, CB, 2, 128], BF16)
    gl = sb("gl", [P, 2, TG])
    ga = sb("ga", [P, 2, TG], BF16)

    ps = [nc.alloc_psum_tensor("ps%d" % i, [P, 512], F32) for i in range(8)]

    def PB(i):
        return ("ps", i)

    pg = Prog(nc)
    sink = [None]

    def A(*args, **kw):
        if sink[0] is None:
            pg.add(*args, **kw)
        else:
            sink[0].append((args, kw))

    def merge(l1, l2, lead=0):
        n1, n2 = len(l1), len(l2)
        k2 = 0
        lead = min(lead, max(n1 - 1, 0))
        for k1, (a_, kw_) in enumerate(l1):
            pg.add(*a_, **kw_)
            tgt = ((k1 + 1 - lead) * n2) // (n1 - lead) if k1 >= lead else 0
            while k2 < tgt:
                pg.add(*l2[k2][0], **l2[k2][1])
                k2 += 1
        while k2 < n2:
            pg.add(*l2[k2][0], **l2[k2][1])
            k2 += 1


    ident = cm[:, 13, :]

    def gview(byte_off, nbytes, dt):
        e0, n = byte_off // 2, nbytes // 2
        v = G[:, e0:e0 + n]
        return v if dt == BF16 else v.bitcast(dt)

    NSR = 3
    ustage = [(gview(o, 4096, F32), gkeys(o, 4096)) for o in range(0, 12288, 4096)]
    vstage = [(gview(o, 4096, F32), gkeys(o, 4096)) for o in range(12288, 24576, 4096)]
    utbuf = [(gview(o, 2048, BF16), gkeys(o, 2048)) for o in range(24576, 30720, 2048)]
    vbuf = [(gview(o, 2048, BF16), gkeys(o, 2048)) for o in range(30720, 36864, 2048)]
    wstage = [(gview(o, 8192, F32), gkeys(o, 8192)) for o in (36864, 45056)]
    wtmp = [(gview(o, 4096, BF16), gkeys(o, 4096)) for o in (53248, 57344)]
    poolw32 = gview(61440, 2048, F32).rearrange("p (g d) -> p g d", g=4)
    sguw32 = gview(63488, 2048, F32).rearrange("p (g d) -> p g d", g=4)
    def gtmp(off, shape, dt=F32):
        n = 4 * int(np.prod(shape))
        v = gview(off, n, dt)
        names = "abcde"[:len(shape)]
        pat = "p (" + " ".join(names) + ") -> p " + " ".join(names)
        return v.rearrange(pat, **{k: int(d) for k, d in zip(names, shape)}) if len(shape) > 1 else v
    top = gtmp(32768, [2, 8, 2, 16])
    idx = gtmp(34816, [2, 8, 2, 16], U32)
    idxF = gtmp(36864, [8, 2, 16])
    s2r = gtmp(37888, [2, 128])
    cand2 = gtmp(38912, [2, 256])
    cv = gtmp(40960, [8, 16])
    ci = gtmp(41984, [8, 16], U32)
    ge = gtmp(43008, [8, 16])
    gate = gtmp(44032, [8, 16])
    gs = gtmp(45056, [16])
    selu = gtmp(46080, [2, 128], U32)
    selF = gtmp(47104, [2, 128])
    abF = gtmp(48128, [2, 128])
    BTMPK = gkeys(32768, 16384)
    BKEYS = ([("top", j, c) for j in range(2) for c in range(16)] + [("idx", j, c) for j in range(2) for c in range(16)]
             + [("s2r", r) for r in range(2)] + [("cand2", r) for r in range(2)] + [("cv", h) for h in range(8)]
             + [("ci", h) for h in range(8)] + ["ge", "gs", "gate", "selu", "selF", "idxF", ("abF", 0), ("abF", 1)])
    bar = sb("bar", [P, 8])
    PW32K = gkeys(61440, 2048)
    SW32K = gkeys(63488, 2048)

    def ld(dst_ap, src_ap, key):
        A("sp", lambda e: e.dma_start(out=dst_ap, in_=src_ap), writes=[key], dma_key=key)

    ld(cm[:], cm_d, "cm")
    ld(rows[:], rows_d, "rows")
    ld(nfin[:], nfin_d, "nfin")
    ld(ncols[:], ncols_d, "ncols")
    A("sp", lambda e: e.dma_start(out=poolw32, in_=poolw_d), writes=PW32K, dma_key="poolw32")
    A("sp", lambda e: e.dma_start(out=sguw32, in_=sguwT_d), writes=SW32K, dma_key="sguw32")
    ld(sgub[:], sgub_d, "sgub")
    ld(keysT[:], keysT_d, "keysT")
    A("pool", lambda e: e.iota(iota[:], pattern=[[1, 128]], base=0, channel_multiplier=0,
                               allow_small_or_imprecise_dtypes=True), writes=["iota"])
    mhalf = sb("mhalf", [P, 4])
    A("pool", lambda e: e.memset(mhalf[:], -0.5), writes=["mhalf"])
    A("dve", lambda e: e.tensor_copy(out=identb[:], in_=ident), reads=["cm"], writes=["identb"])
    A("dve", lambda e: e.tensor_copy(out=iota_rep[:], in_=iota[:].unsqueeze(1).unsqueeze(1).to_broadcast(
        [P, CB, 2, 128])), reads=["iota"], writes=["iota_rep"])
    A("dve", lambda e: e.tensor_tensor(out=poolwb[:], in0=poolw32,
                                       in1=rows[:, 0, :].rearrange("p (g d) -> p g d", g=4), op=ALU.mult),
      reads=PW32K + ["rows"], writes=["poolwb"])
    A("dve", lambda e: e.tensor_tensor(out=sguwb[:], in0=sguw32,
                                       in1=cm[:, 12, :].unsqueeze(1).to_broadcast([P, 4, 128]), op=ALU.mult),
      reads=SW32K + ["cm"], writes=["sguwb"])

    def load_scaled(src_d, ncol0, width, dst_fn, i0):
        for kk in range(8):
            stg, skeys = wstage[(i0 + kk) % 2]
            sv = stg[:, 0:width]
            A("sp", lambda e, sv=sv, kk=kk: e.dma_start(out=sv, in_=src_d[kk * 128:(kk + 1) * 128, :]),
              writes=skeys, dma_key=("wstage", (i0 + kk) % 2))
            dst_fn(kk, sv, skeys)

    def win_dst(kk, sv, skeys):
        A("dve", lambda e: e.tensor_scalar(out=winb[:, kk, :], in0=sv, scalar1=ncols[:, kk:kk + 1], scalar2=None,
                                           op0=ALU.mult), reads=skeys + ["ncols"], writes=[("winb", kk)])

    def wout_dst(kk, sv, skeys):
        A("dve", lambda e: e.tensor_scalar(out=woutb[:, kk, :], in0=sv, scalar1=ncols[:, 8 + kk:9 + kk],
                                           scalar2=None, op0=ALU.mult),
          reads=skeys + ["ncols"], writes=[("woutb", kk)])

    def wq_dst(kk, sv, skeys):
        tv, tkeys = wtmp[kk % 2]
        A("dve", lambda e: e.tensor_scalar(out=tv, in0=sv, scalar1=ncols[:, 16 + kk:17 + kk], scalar2=None,
                                           op0=ALU.mult), reads=skeys + ["ncols"], writes=tkeys)
        A("sp", lambda e: e.dma_start(out=wqs_d[:, :, kk, :].rearrange("q p n -> p q n"),
                                      in_=tv.rearrange("p (q n) -> p q n", q=4)), reads=tkeys, writes=["wqs"],
          dma_key=("wtmp", kk % 2))

    lw = []
    sink[0] = lw
    load_scaled(win_d, 0, 1536, win_dst, 0)
    load_scaled(wout_d, 8, 1024, wout_dst, 0)
    load_scaled(wq_d, 16, 2048, wq_dst, 0)
    lu = []
    sink[0] = lu

    def setup_load(a):
        ar = a % NSR
        us, uk = ustage[ar]
        vsg, vk = vstage[ar]
        A("sp", lambda e, us=us, a=a: e.dma_start(out=us, in_=pu_d[a * 128:(a + 1) * 128, :]),
          writes=uk, dma_key=("ustage", ar))
        A("sp", lambda e, vsg=vsg, a=a: e.dma_start(out=vsg, in_=pv_d[a * 128:(a + 1) * 128, :]),
          writes=vk, dma_key=("vstage", ar))

    for a in range(NSR - 1):
        setup_load(a)
    for a in range(NA):
        ar = a % NSR
        us, uk = ustage[ar]
        vsg, vk = vstage[ar]
        ub, ubk = utbuf[ar]
        vb, vbk = vbuf[ar]
        if a + NSR - 1 < NA:
            setup_load(a + NSR - 1)
        for kk in range(8):
            bank = 4 + (kk // 4) + 2 * (a % 2)
            A("pe", lambda e, us=us, kk=kk, bank=bank: e.transpose(
                out=ps[bank][:, (kk % 4) * 128:(kk % 4 + 1) * 128], in_=us[:, kk * 128:(kk + 1) * 128],
                identity=ident), reads=uk + ["cm"], excl=[PB(bank)])
        for hf in range(2):
            bank = 4 + hf + 2 * (a % 2)
            A("dve", lambda e, ub=ub, hf=hf, bank=bank: e.tensor_tensor(
                out=ub[:, hf * 512:(hf + 1) * 512].rearrange("p (k n) -> p k n", k=4),
                in0=ps[bank][:].rearrange("p (k n) -> p k n", k=4),
                in1=ncols[:, 16 + hf * 4:20 + hf * 4].unsqueeze(2).to_broadcast([P, 4, 128]), op=ALU.mult),
              reads=["ncols"], writes=ubk, excl=[PB(bank)])
        A("sp", lambda e, ub=ub, a=a: e.dma_start(out=uts_d[a].rearrange("p k n -> p (k n)"), in_=ub),
          reads=ubk, writes=[("uts", a)], dma_key=("utbuf", ar))
        A("act", lambda e, vb=vb, vsg=vsg: e.copy(out=vb, in_=vsg), reads=vk, writes=vbk)
        A("sp", lambda e, vb=vb, a=a: e.dma_start(out=vs_d[a], in_=vb), reads=vbk, writes=[("vs", a)],
          dma_key=("vbuf", ar))

    sink[0] = None
    merge(lu, lw)

    ALLG = [("G", b) for b in range(64)]
    WQK = [("G", b) for b in range(32)]
    wqb = G[:, 0:16384].rearrange("p (q k n) -> p q k n", q=4, k=8)

    def rstd_from(sum_col, scale, cols):
        c0, c1, c2 = cols
        A("dve", lambda e: e.tensor_scalar(out=st[:, c0:c0 + 1], in0=st[:, sum_col:sum_col + 1], scalar1=scale,
                                           scalar2=EPS, op0=ALU.mult, op1=ALU.add),
          reads=[("st", sum_col)], writes=[("st", c0)])
        A("pool", lambda e: e.tensor_tensor(out=st[:, c2:c2 + 1], in0=st[:, c0:c0 + 1], in1=mhalf[:, 0:1],
                                            op=ALU.pow), reads=[("st", c0), "mhalf"], writes=[("st", c2)])

    def transpose8(src16, src_keys, dst_ap, dst_keys):
        pb = ps[3][:].bitcast(BF16)
        for kk in range(8):
            A("pe", lambda e, kk=kk: e.transpose(out=pb[:, kk * 128:(kk + 1) * 128],
                                                 in_=src16[:, kk * 128:(kk + 1) * 128], identity=identb[:]),
              reads=src_keys + ["identb"], excl=[PB(3)])
        A("act", lambda e: e.copy(out=dst_ap, in_=pb.rearrange("p (k n) -> p k n", k=8)), excl=[PB(3)],
          writes=dst_keys)

    def phase_A(g):
        tap = dbg and g == 0
        par = g % 2
        xg = xg2[:, par]
        h2T = h2T2[:, par]
        H2K = [("h2T", par, 0), ("h2T", par, 1)]
        ALLG = [("G", b) for b in range(64)]
        for j in range(2):
            ti = g * 2 + j
            XJ = ("xg", par, j)
            xj = xg[:, j, :]
            A("sp", lambda e, xj=xj, ti=ti: e.dma_start(out=xj, in_=x_d[ti * 128:(ti + 1) * 128, :]),
              writes=[XJ], dma_key=XJ)
            A("act", lambda e, xj=xj: e.activation(out=tb16[:], in_=xj, func=AF.Square, accum_out=st[:, 0:1]),
              reads=[XJ], writes=["tb16", ("st", 0)])
            rstd_from(0, 1.0 / D, (1, 2, 3))
            A("act", lambda e, xj=xj: e.activation(out=tb16[:], in_=xj, func=AF.Copy, scale=st[:, 3:4]),
              reads=[XJ, ("st", 3)], writes=["tb16"])
            transpose8(tb16, ["tb16"], hT[:], ["hT"])
            for kk in range(8):
                for n in range(3):
                    A("pe", lambda e, kk=kk, n=n: e.matmul(ps[n][:], lhsT=hT[:, kk, :],
                                                           rhs=winb[:, kk, n * 512:(n + 1) * 512],
                                                           start=(kk == 0), stop=(kk == 7)),
                      reads=["hT", ("winb", kk)], excl=[PB(n)])
            cur, prv = ti % 2, (ti + 1) % 2
            A("act", lambda e, cur=cur: e.copy(out=p_sb[:, cur, :], in_=ps[0][:]), excl=[PB(0)],
              writes=[("p_sb", cur)])
            A("act", lambda e: e.activation(out=u_sb[:], in_=ps[1][:], func=AF.Gelu), excl=[PB(1)],
              writes=["u_sb"])
            A("act", lambda e: e.activation(out=v_sb[:], in_=ps[2][:], func=AF.Gelu), excl=[PB(2)],
              writes=["v_sb"])
            for gg in range(4):
                first = (ti == 0)
                A("pe", lambda e, gg=gg, cur=cur, first=first: e.matmul(
                    ps[4][:, gg * 128:(gg + 1) * 128], lhsT=p_sb[:, cur, gg * 128:(gg + 1) * 128],
                    rhs=cm[:, (8 + gg) if first else gg, :], start=True, stop=first),
                  reads=[("p_sb", cur), "cm"], excl=[PB(4)])
                if not first:
                    A("pe", lambda e, gg=gg, prv=prv: e.matmul(
                        ps[4][:, gg * 128:(gg + 1) * 128], lhsT=p_sb[:, prv, gg * 128:(gg + 1) * 128],
                        rhs=cm[:, 4 + gg, :], start=False, stop=True),
                      reads=[("p_sb", prv), "cm"], excl=[PB(4)])
            A("act", lambda e: e.copy(out=dTb[:], in_=ps[4][:]), excl=[PB(4)], writes=["dTb"])
            for gg in range(4):
                A("pe", lambda e, gg=gg: e.matmul(ps[5][:, gg * 128:(gg + 1) * 128],
                                                  lhsT=dTb[:, gg * 128:(gg + 1) * 128], rhs=poolwb[:, gg, :],
                                                  start=True, stop=True),
                  reads=["dTb", "poolwb"], excl=[PB(5)])
            v3 = v_sb[:].rearrange("p (h c) -> p h c", h=4)
            tA3 = tmpA[:].rearrange("p (h c) -> p h c", h=4)
            A("dve", lambda e: e.tensor_reduce(out=st[:, 16:20], in_=v3, axis=AX.X, op=ALU.add),
              reads=["v_sb"], writes=[("st", 16)])
            A("act", lambda e: e.activation(out=tmpA[:], in_=v_sb[:], func=AF.Square), reads=["v_sb"],
              writes=["tmpA"])
            A("dve", lambda e: e.tensor_reduce(out=st[:, 20:24], in_=tA3, axis=AX.X, op=ALU.add),
              reads=["tmpA"], writes=[("st", 20)])
            A("dve", lambda e: e.tensor_scalar(out=st[:, 24:28], in0=st[:, 16:20], scalar1=1.0 / 128, scalar2=None,
                                               op0=ALU.mult), reads=[("st", 16)], writes=[("st", 24)])
            A("dve", lambda e: e.tensor_tensor(out=st[:, 28:32], in0=st[:, 24:28], in1=st[:, 24:28], op=ALU.mult),
              reads=[("st", 24)], writes=[("st", 28)])
            A("dve", lambda e: e.scalar_tensor_tensor(out=st[:, 32:36], in0=st[:, 20:24], scalar=1.0 / 128,
                                                      in1=st[:, 28:32], op0=ALU.mult, op1=ALU.subtract),
              reads=[("st", 20), ("st", 28)], writes=[("st", 32)])
            A("dve", lambda e: e.tensor_scalar(out=st[:, 36:40], in0=st[:, 32:36], scalar1=EPS, scalar2=None,
                                               op0=ALU.add), reads=[("st", 32)], writes=[("st", 36)])
            A("pool", lambda e: e.tensor_tensor(out=st[:, 44:48], in0=st[:, 36:40], in1=mhalf[:, 0:4], op=ALU.pow),
              reads=[("st", 36), "mhalf"], writes=[("st", 44)])
            A("dve", lambda e: e.tensor_tensor(out=tA3, in0=v3,
                                               in1=st[:, 24:28].unsqueeze(2).to_broadcast([P, 4, 128]),
                                               op=ALU.subtract), reads=["v_sb", ("st", 24)], writes=["tmpA"])
            A("dve", lambda e: e.tensor_tensor(out=tA3, in0=tA3,
                                               in1=st[:, 44:48].unsqueeze(2).to_broadcast([P, 4, 128]),
                                               op=ALU.mult), reads=[("st", 44)], writes=["tmpA"])
            A("dve", lambda e: e.tensor_tensor(out=tmpA[:], in0=tmpA[:], in1=rows[:, 1, :], op=ALU.mult),
              reads=["rows"], writes=["tmpA"])
            A("dve", lambda e: e.tensor_tensor(out=vnb[:], in0=tmpA[:], in1=rows[:, 2, :], op=ALU.add),
              reads=["rows", "tmpA"], writes=["vnb"])
            for hh in range(4):
                A("pe", lambda e, hh=hh: e.matmul(ps[4][:, hh * 128:(hh + 1) * 128], lhsT=sguwb[:, hh, :],
                                                  rhs=vnb[:, hh * 128:(hh + 1) * 128], start=True, stop=True),
                  reads=["sguwb", "vnb"], excl=[PB(4)])
            tB3 = tmpB[:].rearrange("p (h c) -> p h c", h=4)
            A("dve", lambda e: e.tensor_tensor(out=tB3, in0=ps[4][:].rearrange("p (h c) -> p h c", h=4),
                                               in1=sgub[:].unsqueeze(2).to_broadcast([P, 4, 128]), op=ALU.add),
              reads=["sgub"], writes=["tmpB"], excl=[PB(4)])
            A("dve", lambda e: e.tensor_tensor(out=tmpB[:], in0=tmpB[:], in1=u_sb[:], op=ALU.mult),
              reads=["u_sb"], writes=["tmpB"])
            if tap and j == 0:
                A("dve", lambda e: e.tensor_copy(out=tmpA[:], in_=ps[5][:]), excl=[PB(5)], writes=["tmpA"])
                A("sp", lambda e: e.dma_start(out=dbg_d["d_aout"], in_=tmpA[:]), reads=["tmpA"], dma_key="dbg1")
                A("sp", lambda e: e.dma_start(out=dbg_d["d_bout"], in_=tmpB[:]), reads=["tmpB"], dma_key="dbg2")
            A("act", lambda e: e.activation(out=tb16[:, 0:512], in_=ps[5][:], func=AF.Square,
                                            accum_out=st[:, 4:5]), excl=[PB(5)], writes=["tb16", ("st", 4)])
            A("act", lambda e: e.activation(out=tb16[:, 512:1024], in_=tmpB[:], func=AF.Square,
                                            accum_out=st[:, 5:6]), reads=["tmpB"], writes=["tb16", ("st", 5)])
            A("dve", lambda e: e.tensor_scalar(out=st[:, 6:8], in0=st[:, 4:6], scalar1=1.0 / 512, scalar2=EPS,
                                               op0=ALU.mult, op1=ALU.add), reads=[("st", 4), ("st", 5)],
              writes=[("st", 6)])
            A("pool", lambda e: e.tensor_tensor(out=st[:, 10:12], in0=st[:, 6:8], in1=mhalf[:, 0:2], op=ALU.pow),
              reads=[("st", 6), "mhalf"], writes=[("st", 10)])
            A("act", lambda e: e.activation(out=tb16[:, 0:512], in_=ps[5][:], func=AF.Copy, scale=st[:, 10:11]),
              reads=[("st", 10)], writes=["tb16"], excl=[PB(5)])
            A("act", lambda e: e.activation(out=tb16[:, 512:1024], in_=tmpB[:], func=AF.Copy, scale=st[:, 11:12]),
              reads=[("st", 10), "tmpB"], writes=["tb16"])
            transpose8(tb16, ["tb16"], hT[:], ["hT"])
            for kk in range(8):
                for n in range(2):
                    A("pe", lambda e, kk=kk, n=n: e.matmul(ps[n][:], lhsT=hT[:, kk, :],
                                                           rhs=woutb[:, kk, n * 512:(n + 1) * 512],
                                                           start=(kk == 0), stop=(kk == 7)),
                      reads=["hT", ("woutb", kk)], excl=[PB(n)])
            for n in range(2):
                A("dve", lambda e, n=n, j=j: e.tensor_tensor(out=xg[:, j, n * 512:(n + 1) * 512], in0=ps[n][:],
                                                             in1=xg[:, j, n * 512:(n + 1) * 512], op=ALU.add),
                  writes=[XJ], excl=[PB(n)])
            if tap:
                A("sp", lambda e, xj=xj, j=j: e.dma_start(out=dbg_d["d_x2"][j * 128:(j + 1) * 128, :], in_=xj),
                  reads=[XJ], dma_key=("dbgx", j))
            A("act", lambda e, xj=xj: e.activation(out=tb16[:], in_=xj, func=AF.Square, accum_out=st[:, 12:13]),
              reads=[XJ], writes=["tb16", ("st", 12)])
            rstd_from(12, 1.0 / D, (13, 14, 15))
            A("act", lambda e, xj=xj: e.activation(out=tb16[:], in_=xj, func=AF.Copy, scale=st[:, 15:16]),
              reads=[XJ, ("st", 15)], writes=["tb16"])
            transpose8(tb16, ["tb16"], h2T[:, :, j * 128:(j + 1) * 128], [("h2T", par, j)])
        if tap:
            A("act", lambda e: e.copy(out=arB1[:], in_=h2T[:].rearrange("p k t -> p (k t)")), reads=H2K,
              writes=[("arB1", s) for s in range(4)])
            A("sp", lambda e: e.dma_start(out=dbg_d["d_h2T"], in_=arB1[:]), reads=[("arB1", s) for s in range(4)],
              dma_key="dbg3")

    def phase_B(g):
        tap = dbg and g == 0
        par = g % 2
        xg = xg2[:, par]
        h2T = h2T2[:, par]
        H2K = [("h2T", par, 0), ("h2T", par, 1)]
        ALLG = [("G", b) for b in range(64)]
        A("dve", lambda e: e.memset(bar[:, 0:1], 0.0), reads=BTMPK, writes=BTMPK + BKEYS)
        AB1 = [("arB1", s) for s in range(4)]
        AB2 = [("arB2", s) for s in range(4)]
        for qq in range(4):
            qs = qq % 2
            qTq = arB2[:, qs * 1024:(qs + 1) * 1024].rearrange("p (c t) -> p c t", c=4)
            s_q = arB1[:, qs * 1024:(qs + 1) * 1024].rearrange("p (j c k) -> p j c k", j=2, c=4)
            for cc in range(4):
                c = qq * 4 + cc
                for kk in range(8):
                    A("pe", lambda e, cc=cc, c=c, kk=kk, qq=qq: e.matmul(
                        ps[6 + cc // 2][:, (cc % 2) * 256:(cc % 2 + 1) * 256],
                        lhsT=wqb[:, qq, kk, cc * 128:(cc + 1) * 128], rhs=h2T[:, kk, :], start=(kk == 0),
                        stop=(kk == 7)),
                      reads=[("G", b) for b in range(qq * 8, qq * 8 + 8)] + H2K, excl=[PB(6 + cc // 2)])
            for b2 in range(2):
                A("act", lambda e, b2=b2, qs=qs: e.copy(
                    out=arB2[:, qs * 1024 + b2 * 512: qs * 1024 + (b2 + 1) * 512], in_=ps[6 + b2][:]),
                  excl=[PB(6 + b2)], writes=[("arB2", qs * 2 + b2)])
            for j in range(2):
                for cc in range(4):
                    c = qq * 4 + cc
                    A("pe", lambda e, cc=cc, c=c, j=j, qTq=qTq: e.matmul(
                        ps[6 + j][:, cc * 128:(cc + 1) * 128], lhsT=qTq[:, cc, j * 128:(j + 1) * 128],
                        rhs=keysT[:, c % 2, :], start=True, stop=True),
                      reads=[("arB2", qs * 2 + cc // 2), "keysT"], excl=[PB(6 + j)])
                A("act", lambda e, j=j, qs=qs: e.copy(
                    out=arB1[:, qs * 1024 + j * 512: qs * 1024 + (j + 1) * 512], in_=ps[6 + j][:]),
                  excl=[PB(6 + j)], writes=[("arB1", qs * 2 + j)])
            for j in range(2):
                for cp in range(2):
                    steps = [[], [], [], [], []]
                    for cc in (2 * cp, 2 * cp + 1):
                        c = qq * 4 + cc
                        hh, pp = c // 2, c % 2
                        sk = [("arB1", qs * 2 + j)]
                        src = s_q[:, j, cc, :]
                        tk = [("top", j, c)]
                        ik = [("idx", j, c)]
                        rr = c % 2
                        steps[0].append(("dve", lambda e, src=src, j=j, hh=hh, pp=pp: e.max(
                            out=top[:, j, hh, pp, 0:8], in_=src), dict(reads=sk, writes=tk)))
                        steps[1].append(("dve", lambda e, src=src, j=j, hh=hh, pp=pp: e.max_index(
                            out=idx[:, j, hh, pp, 0:8], in_max=top[:, j, hh, pp, 0:8], in_values=src),
                            dict(reads=sk + tk, writes=ik)))
                        steps[2].append(("dve", lambda e, src=src, j=j, hh=hh, pp=pp, rr=rr: e.match_replace(
                            out=s2r[:, rr, :], in_to_replace=top[:, j, hh, pp, 0:8], in_values=src, imm_value=NEG),
                            dict(reads=sk + tk, writes=[("s2r", rr)])))
                        steps[3].append(("dve", lambda e, j=j, hh=hh, pp=pp, rr=rr: e.max(
                            out=top[:, j, hh, pp, 8:16], in_=s2r[:, rr, :]), dict(reads=[("s2r", rr)], writes=tk)))
                        steps[4].append(("dve", lambda e, j=j, hh=hh, pp=pp, rr=rr: e.max_index(
                            out=idx[:, j, hh, pp, 8:16], in_max=top[:, j, hh, pp, 8:16], in_values=s2r[:, rr, :]),
                            dict(reads=[("s2r", rr)] + tk, writes=ik)))
                    for stp in steps:
                        for (en_, fn_, kw_) in stp:
                            A(en_, fn_, **kw_)
        if tap:
            A("sp", lambda e: e.dma_start(out=dbg_d["d_top"], in_=top[:].rearrange("p j h q k -> p (j h q k)")),
              reads=[("top", j, c) for j in range(2) for c in range(16)], dma_key="dbg4")

        for j in range(2):
            TK = [("top", j, c) for c in range(16)]
            IK = [("idx", j, c) for c in range(16)]
            cand = arB1[:].rearrange("p (h i k) -> p h i k", h=8, i=16)
            A("dve", lambda e, j=j, cand=cand: e.tensor_tensor(
                out=cand, in0=top[:, j, :, 0, :].unsqueeze(3).to_broadcast([P, 8, 16, 16]),
                in1=top[:, j, :, 1, :].unsqueeze(2).to_broadcast([P, 8, 16, 16]), op=ALU.add),
              reads=TK, writes=AB1)
            cand_f = arB1[:].rearrange("p (h n) -> p h n", h=8)
            for hp in range(4):
                steps = [[], [], [], [], []]
                for hh in (2 * hp, 2 * hp + 1):
                    rr = hh % 2
                    src = cand_f[:, hh, :]
                    steps[0].append((lambda e, src=src, hh=hh: e.max(out=cv[:, hh, 0:8], in_=src),
                                     dict(reads=AB1, writes=[("cv", hh)])))
                    steps[1].append((lambda e, src=src, hh=hh: e.max_index(out=ci[:, hh, 0:8], in_max=cv[:, hh, 0:8],
                                                                          in_values=src),
                                     dict(reads=AB1 + [("cv", hh)], writes=[("ci", hh)])))
                    steps[2].append((lambda e, src=src, hh=hh, rr=rr: e.match_replace(
                        out=cand2[:, rr, :], in_to_replace=cv[:, hh, 0:8], in_values=src, imm_value=NEG),
                        dict(reads=AB1 + [("cv", hh)], writes=[("cand2", rr)])))
                    steps[3].append((lambda e, hh=hh, rr=rr: e.max(out=cv[:, hh, 8:16], in_=cand2[:, rr, :]),
                                     dict(reads=[("cand2", rr)], writes=[("cv", hh)])))
                    steps[4].append((lambda e, hh=hh, rr=rr: e.max_index(out=ci[:, hh, 8:16], in_max=cv[:, hh, 8:16],
                                                                        in_values=cand2[:, rr, :]),
                                     dict(reads=[("cand2", rr), ("cv", hh)], writes=[("ci", hh)])))
                for stp in steps:
                    for (fn_, kw_) in stp:
                        A("dve", fn_, **kw_)
            CVK = [("cv", h) for h in range(8)]
            CIK = [("ci", h) for h in range(8)]
            if tap and j == 0:
                A("sp", lambda e: e.dma_start(out=dbg_d["d_cv"], in_=cv[:].rearrange("p h k -> p (h k)")),
                  reads=CVK, dma_key="dbg5")
            A("dve", lambda e: e.tensor_tensor(out=ge[:], in0=cv[:], in1=cv[:, :, 0:1].to_broadcast([P, 8, 16]),
                                               op=ALU.subtract), reads=CVK, writes=["ge"])
            A("act", lambda e: e.activation(out=ge[:], in_=ge[:], func=AF.Exp), writes=["ge"])
            A("dve", lambda e: e.tensor_reduce(out=gs[:, 0:8], in_=ge[:], axis=AX.X, op=ALU.add), reads=["ge"],
              writes=["gs"])
            A("dve", lambda e: e.reciprocal(out=gs[:, 8:16], in_=gs[:, 0:8]), writes=["gs"])
            A("dve", lambda e: e.tensor_tensor(out=gate[:], in0=ge[:],
                                               in1=gs[:, 8:16].unsqueeze(2).to_broadcast([P, 8, 16]), op=ALU.mult),
              reads=["ge", "gs"], writes=["gate"])
            ci_f = ci[:].rearrange("p h k -> p (h k)")
            A("dve", lambda e: e.tensor_single_scalar(out=selu[:, 0, :], in_=ci_f, scalar=4,
                                                      op=ALU.logical_shift_right), reads=CIK, writes=["selu"])
            A("dve", lambda e: e.tensor_single_scalar(out=selu[:, 1, :], in_=ci_f, scalar=15, op=ALU.bitwise_and),
              reads=CIK, writes=["selu"])
            A("dve", lambda e: e.tensor_copy(out=selF[:], in_=selu[:]), reads=["selu"], writes=["selF"])
            A("dve", lambda e, j=j: e.tensor_copy(out=idxF[:], in_=idx[:, j]), reads=IK, writes=["idxF"])
            ohi = arB1[:].rearrange("p (n i) -> p n i", i=16)
            ohi4 = arB1[:].rearrange("p (h k i) -> p h k i", h=8, k=16)
            for pp in range(2):
                A("dve", lambda e, pp=pp: e.tensor_tensor(
                    out=ohi, in0=selF[:, pp, :].unsqueeze(2).to_broadcast([P, 128, 16]),
                    in1=iota[:, 0:16].unsqueeze(1).to_broadcast([P, 128, 16]), op=ALU.is_equal),
                  reads=["selF", "iota"], writes=AB1)
                A("dve", lambda e, pp=pp: e.tensor_tensor(
                    out=ohi4, in0=ohi4, in1=idxF[:, :, pp, :].unsqueeze(2).to_broadcast([P, 8, 16, 16]),
                    op=ALU.mult), reads=["idxF"], writes=AB1)
                A("dve", lambda e, pp=pp: e.tensor_reduce(out=abF[:, pp, :], in_=ohi, axis=AX.X, op=ALU.add),
                  reads=AB1, writes=[("abF", pp)])
            srcs = [abF[:, 0, :], abF[:, 1, :], gate[:].rearrange("p h k -> p (h k)")]
            for q3 in range(3):
                A("pe", lambda e, q3=q3: e.transpose(out=ps[6][:, q3 * 128:(q3 + 1) * 128], in_=srcs[q3],
                                                     identity=ident),
                  reads=[("abF", 0), ("abF", 1), "gate", "cm"], excl=[PB(6)])
            A("act", lambda e, j=j: e.copy(out=abg[:, :, j * 128:(j + 1) * 128],
                                           in_=ps[6][:, 0:384].rearrange("p (q t) -> p q t", q=3)),
              excl=[PB(6)], writes=[("abg", j)])
        if tap:
            A("sp", lambda e: e.dma_start(out=dbg_d["d_abg"], in_=abg[:].rearrange("p q t -> p (q t)")),
              reads=[("abg", 0), ("abg", 1)], dma_key="dbg6")

        A("dve", lambda e: e.memset(bar[:, 1:2], 0.0), writes=BTMPK + BKEYS)
    def phase_C(g):
        tap = dbg and g == 0
        par = g % 2
        xg = xg2[:, par]
        h2T = h2T2[:, par]
        H2K = [("h2T", par, 0), ("h2T", par, 1)]
        ALLG = [("G", b) for b in range(64)]
        NB = TG // CB
        PREP_EVERY = 3

        def c_prep(tb):
            if tb % PREP_EVERY:
                return
            t0 = tb * CB
            r = (tb // PREP_EVERY) % RR
            jt = t0 // 128
            src = abg[:, :, t0:t0 + CB].rearrange("p q t -> p t q").unsqueeze(3).to_broadcast([P, CB, 3, 128])
            A("act", lambda e: e.copy(out=rep[:, r], in_=src), reads=[("abg", jt)], writes=[("rep", r)])

        def c_compare(tb):
            r = tb % CR
            t0 = tb * CB
            jt = t0 // 128
            if tb % PREP_EVERY == 0:
                rr = (tb // PREP_EVERY) % RR
                A("dve", lambda e: e.tensor_tensor(out=OH[:, r], in0=iota_rep[:], in1=rep[:, rr, :, 0:2, :],
                                                   op=ALU.is_equal),
                  reads=["iota_rep", ("rep", rr)], writes=[("OH", r)])
            else:
                sel_ab = abg[:, 0:2, t0:t0 + CB].rearrange("p q t -> p t q").unsqueeze(3).to_broadcast(
                    [P, CB, 2, 128])
                A("dve", lambda e: e.tensor_tensor(out=OH[:, r], in0=iota_rep[:], in1=sel_ab, op=ALU.is_equal),
                  reads=["iota_rep", ("abg", jt)], writes=[("OH", r)])

        def c_scale(tb):
            r = tb % CR
            t0 = tb * CB
            jt = t0 // 128
            if tb % PREP_EVERY == 0:
                rr = (tb // PREP_EVERY) % RR
                A("dve", lambda e: e.tensor_tensor(out=OH[:, r, :, 0, :], in0=OH[:, r, :, 0, :],
                                                   in1=rep[:, rr, :, 2, :], op=ALU.mult),
                  reads=[("rep", rr)], writes=[("OH", r)])
            else:
                A("pool", lambda e: e.tensor_tensor(
                    out=OH[:, r, :, 0, :], in0=OH[:, r, :, 0, :],
                    in1=abg[:, 2, t0:t0 + CB].unsqueeze(2).to_broadcast([P, CB, 128]), op=ALU.mult),
                  reads=[("abg", jt)], writes=[("OH", r)])

        def c_mm(tb):
            t0 = tb * CB
            r = tb % CR
            for tt in range(CB):
                t = t0 + tt
                bank = 6 + (t // 4) % 2
                A("pe", lambda e, t=t, tt=tt, bank=bank: e.matmul(
                    ps[bank][:, (t % 4) * 128:(t % 4 + 1) * 128], lhsT=OH[:, r, tt, 1, :], rhs=OH[:, r, tt, 0, :],
                    start=True, stop=True),
                  reads=[("OH", r)], excl=[PB(bank)])
                if t % 4 == 3:
                    blk = t // 4
                    A("act", lambda e, blk=blk, bank=bank: e.copy(out=G[:, blk * 512:(blk + 1) * 512],
                                                                  in_=ps[bank][:]),
                      excl=[PB(bank)], writes=[("G", blk)])

        for tb in range(min(PREP_EVERY * RR, NB)):
            c_prep(tb)
        c_compare(0)
        for tb in range(NB):
            if tb + 1 < NB:
                c_compare(tb + 1)
            c_scale(tb)
            c_mm(tb)
            if tb + PREP_EVERY * RR < NB:
                c_prep(tb + PREP_EVERY * RR)
        if tap:
            A("act", lambda e: e.copy(out=arB1[:, 0:1024], in_=G[:, 0:1024]), reads=ALLG,
              writes=[("arB1", 0), ("arB1", 1)])
            A("sp", lambda e: e.dma_start(out=dbg_d["d_G"], in_=arB1[:, 0:1024]),
              reads=[("arB1", 0), ("arB1", 1)], dma_key="dbg7")

    def phase_D(g):
        tap = dbg and g == 0
        par = g % 2
        xg = xg2[:, par]
        h2T = h2T2[:, par]
        H2K = [("h2T", par, 0), ("h2T", par, 1)]
        ALLG = [("G", b) for b in range(64)]
        NRG = 5
        ut_v = [arB1[:, s * 512:(s + 1) * 512].bitcast(BF16).rearrange("p (k n) -> p k n", k=8) for s in range(4)]
        ut_v.append(utx[:].bitcast(BF16).rearrange("p (k n) -> p k n", k=8))
        vt_v = [arB2[:, s * 512:(s + 1) * 512].bitcast(BF16) for s in range(4)]
        vt_v.append(vtx[:].bitcast(BF16))
        UK = [("arB1", s) for s in range(4)] + ["utx"]
        VK = [("arB2", s) for s in range(4)] + ["vtx"]

        def load_u(a):
            s = a % NRG
            A("sp", lambda e, a=a, s=s: e.dma_start(out=ut_v[s], in_=uts_d[a]), reads=[("uts", a)],
              writes=[UK[s]], dma_key=("ut", s))

        def load_v(a):
            s = a % NRG
            A("sp", lambda e, a=a, s=s: e.dma_start(out=vt_v[s], in_=vs_d[a]), reads=[("vs", a)],
              writes=[VK[s]], dma_key=("vt", s))

        def d_first(a):
            s = a % NRG
            bank = 4 + a % 2
            for kk in range(8):
                A("pe", lambda e, s=s, kk=kk, bank=bank: e.matmul(ps[bank][:, 0:TG], lhsT=ut_v[s][:, kk, :],
                                                                  rhs=h2T[:, kk, :], start=(kk == 0),
                                                                  stop=(kk == 7)),
                  reads=[UK[s]] + H2K, excl=[PB(bank)])

        for a in range(min(NRG, na_main)):
            load_u(a)
            if a < NRG - 1:
                load_v(a)
        d_first(0)
        for a in range(na_main):
            if a + NRG - 1 < na_main:
                load_v(a + NRG - 1)
            if a + NRG < na_main:
                load_u(a + NRG)
            if a + 1 < na_main:
                d_first(a + 1)
            s = a % NRG
            r2 = a % 2
            bank = 4 + r2
            A("act", lambda e, r2=r2, bank=bank: e.activation(out=gl[:, r2, :], in_=ps[bank][:, 0:TG],
                                                              func=AF.Gelu),
              excl=[PB(bank)], writes=[("gl", r2)])
            A("dve", lambda e, r2=r2, a=a: e.tensor_tensor(out=ga[:, r2, :], in0=gl[:, r2, :],
                                                           in1=G[:, a:TG * 128:128], op=ALU.mult),
              reads=[("gl", r2)] + ALLG, writes=[("ga", r2)])
            for tt in range(2):
                for n in range(2):
                    A("pe", lambda e, s=s, r2=r2, tt=tt, n=n, a=a: e.matmul(
                        ps[tt * 2 + n][:], lhsT=ga[:, r2, tt * 128:(tt + 1) * 128],
                        rhs=vt_v[s][:, n * 512:(n + 1) * 512], start=(a == 0), stop=(a == na_main - 1)),
                      reads=[("ga", r2), VK[s]], excl=[PB(tt * 2 + n)])

    def phase_E(g):
        tap = dbg and g == 0
        par = g % 2
        xg = xg2[:, par]
        h2T = h2T2[:, par]
        H2K = [("h2T", par, 0), ("h2T", par, 1)]
        ALLG = [("G", b) for b in range(64)]
        for j in range(2):
            ti = g * 2 + j
            XJ = ("xg", par, j)
            x3 = atmp[:, 2 * j:2 * j + 2, :].rearrange("p a n -> p (a n)")
            XK = ["u_sb", "v_sb"] if j == 0 else ["tmpA", "tmpB"]
            c0 = 48 + 4 * j
            for n in range(2):
                A("dve", lambda e, n=n, j=j, x3=x3: e.tensor_tensor(out=x3[:, n * 512:(n + 1) * 512],
                                                                    in0=ps[j * 2 + n][:],
                                                                    in1=xg[:, j, n * 512:(n + 1) * 512], op=ALU.add),
                  reads=[XJ], writes=[XK[n]], excl=[PB(j * 2 + n)])
            A("act", lambda e, x3=x3, c0=c0: e.activation(out=tb16[:], in_=x3, func=AF.Square,
                                                          accum_out=st[:, c0:c0 + 1]),
              reads=XK, writes=["tb16", ("st", c0)])
            rstd_from(c0, 1.0 / D, (c0 + 1, c0 + 2, c0 + 3))
            A("dve", lambda e, x3=x3, c0=c0: e.scalar_tensor_tensor(out=x3, in0=x3, scalar=st[:, c0 + 3:c0 + 4],
                                                                    in1=nfin[:], op0=ALU.mult, op1=ALU.mult),
              reads=[("st", c0 + 3), "nfin"], writes=XK)
            A("sp", lambda e, x3=x3, ti=ti: e.dma_start(out=out_d[ti * 128:(ti + 1) * 128, :], in_=x3),
              reads=XK, dma_key=("ost", j))

    def load_wq():
        for qq in range(4):
            A("sp", lambda e, qq=qq: e.dma_start(out=wqb[:, qq], in_=wqs_d[qq]), reads=["wqs"],
              writes=[("G", b) for b in range(qq * 8, qq * 8 + 8)], dma_key=("wqb", qq))

    def merged(fns_main, fn_side):
        lm, ls = [], []
        sink[0] = lm
        for f in fns_main:
            f()
        sink[0] = ls
        if fn_side is not None:
            fn_side()
        sink[0] = None
        if ls:
            merge(lm, ls, lead=46)
        else:
            for a_, kw_ in lm:
                pg.add(*a_, **kw_)

    load_wq()
    phase_A(0)
    merged([lambda: phase_B(0), lambda: phase_C(0)], (lambda: phase_A(1)) if NGRP > 1 else None)
    for g in range(NGRP):
        phase_D(g)
        if g + 1 < NGRP:
            load_wq()
        phase_E(g)
        if g + 1 < NGRP:
            merged([lambda g=g: phase_B(g + 1), lambda g=g: phase_C(g + 1)],
                   (lambda g=g: phase_A(g + 2)) if g + 2 < NGRP else None)

    pg.emit()
    return nc


def _host_consts():
    cm = np.zeros((P, 14, 128), np.float32)
    s = np.arange(128)[:, None]
    t = np.arange(128)[None, :]
    for g, win in enumerate((2, 4, 8, 16)):
        band = ((t - s) >= 0) & ((t - s) <= win - 1)
        cm[:, g, :] = band * (1.0 / win) - (s == t)
        cm[:, 4 + g, :] = ((t - (s - 128)) <= win - 1) * (1.0 / win)
        cnt = np.minimum(t + 1, win).astype(np.float64)
        cm[:, 8 + g, :] = band / cnt - (s == t)
    cm[:, 12, :] = (s <= t)
    cm[:, 13, :] = (s == t)
    return cm


def _prep_shared(inp):
    f = lambda k: np.ascontiguousarray(inp[k], dtype=np.float32)
    col = lambda v: np.ascontiguousarray(v.reshape(8, 128).T)
    ncols = np.concatenate([col(f("norm_mix")[0]),
                            col(np.concatenate([f("out_norm_pool")[0], f("out_norm_sgu")[0]])),
                            col(f("norm_ffn")[0])], axis=1)
    rows512 = np.stack([f("pool_scale")[0].reshape(512), f("sgu_ln_g")[0].reshape(512),
                        f("sgu_ln_b")[0].reshape(512)])
    shared = {
        "w_in": f("w_in")[0], "w_out": f("w_out")[0], "wq": f("peer_wq")[0],
        "peer_u": f("peer_u")[0], "peer_v": f("peer_v")[0],
        "ncols": np.ascontiguousarray(ncols),
        "rows512": np.ascontiguousarray(np.broadcast_to(rows512[None], (P, 3, 512))),
        "nfin": np.ascontiguousarray(np.broadcast_to(f("norm_final")[None, :], (P, D))),
        "poolw": np.ascontiguousarray(f("pool_w")[0].transpose(1, 0, 2)),
        "sguwT": np.ascontiguousarray(f("sgu_w")[0].transpose(2, 0, 1)),
        "sgub": np.ascontiguousarray(f("sgu_b")[0].T),
        "keysT": np.ascontiguousarray(f("peer_keys")[0].transpose(2, 0, 1)),
        "cm": _host_consts(),
    }
    return shared


def kernel(**inputs):
    x = np.ascontiguousarray(inputs["x"], dtype=np.float32)
    B, S, _ = x.shape
    shared = _prep_shared(inputs)
    nc = build_nc(S)
    in_maps = []
    for b in range(B):
        m = dict(shared)
        m["x"] = x[b]
        in_maps.append(m)
    res = run_bass_kernel_spmd(nc, in_maps, core_ids=list(range(B)))
    return np.stack([np.asarray(r["out"], dtype=np.float32) for r in res.results], axis=0)
```
